# Optimizing a Trainium2 kernel written in Bass

```python
import jax
import jax.numpy as jnp
from jax import lax
import numpy as np

D_MODEL = 1024
BATCH = 16
SEQ = 4096
DEPTH = 2
DEC_BATCH = 32
DEC_SEQ = 16
PAST_LEN = 2048

CHUNK = 64
Q_BLOCK = 128
EPS = 1e-6
NEG_INF = -1e30

LRU_WIDTH = D_MODEL
LRU_BLOCKS = 8
LRU_BLOCK_W = LRU_WIDTH // LRU_BLOCKS
CONV_W = 4
LRU_C = 8.0

MLA_HEADS = 8
QK_NOPE = 128
QK_ROPE = 64
V_DIM = 128
Q_RANK = 768
KV_RANK = 256
MLA_WIDTH = MLA_HEADS * V_DIM
ROPE_THETA = 10000.0
SOFTMAX_SCALE = (QK_NOPE + QK_ROPE) ** -0.5

IN_COLS = (LRU_WIDTH, LRU_WIDTH, Q_RANK, KV_RANK, QK_ROPE, MLA_WIDTH, D_MODEL, D_MODEL)
IN_DIM = LRU_WIDTH * 2 + Q_RANK + KV_RANK + QK_ROPE + MLA_WIDTH + D_MODEL * 2

kernel_name = 'hybrid_rglru_mla_stream_step'


def _rmsnorm(x, g):
    x32 = x.astype(jnp.float32)
    y = x32 * lax.rsqrt(jnp.mean(x32 * x32, axis=-1, keepdims=True) + EPS)
    return (y * g.astype(jnp.float32)).astype(x.dtype)


def _split_cols(z):
    outs = []
    start = 0
    for w in IN_COLS:
        outs.append(z[..., start:start + w])
        start += w
    return outs


def _rope(x, pos):
    half = QK_ROPE // 2
    inv = ROPE_THETA ** (-jnp.arange(half, dtype=jnp.float32) / half)
    ang = pos.astype(jnp.float32)[:, None] * inv[None, :]
    cos = jnp.cos(ang)[None, :, None, :]
    sin = jnp.sin(ang)[None, :, None, :]
    x32 = x.astype(jnp.float32)
    x1, x2 = x32[..., :half], x32[..., half:]
    return jnp.concatenate([x1 * cos - x2 * sin, x1 * sin + x2 * cos], axis=-1).astype(x.dtype)


def _rglru(xc, h0, wa, ba, wx, bx, lam):
    B, S, W = xc.shape
    xb = xc.reshape(B, S, LRU_BLOCKS, LRU_BLOCK_W)
    r = jax.nn.sigmoid(jnp.einsum('bsnk,nkj->bsnj', xb, wa).reshape(B, S, W) + ba)
    i = jax.nn.sigmoid(jnp.einsum('bsnk,nkj->bsnj', xb, wx).reshape(B, S, W) + bx)
    log_a = (-LRU_C * r.astype(jnp.float32)) * jax.nn.softplus(-lam.astype(jnp.float32))
    a = jnp.exp(log_a)
    mult = jnp.sqrt(-jnp.expm1(2.0 * log_a))
    b = mult * (i * xc).astype(jnp.float32)
    b = b.at[:, 0].add(a[:, 0] * h0.astype(jnp.float32))

    def combine(left, right):
        a_l, b_l = left
        a_r, b_r = right
        return a_l * a_r, a_r * b_l + b_r

    _, hs = lax.associative_scan(combine, (a, b), axis=1)
    return hs.astype(xc.dtype), hs[:, -1].astype(xc.dtype)


def _mla_attention(q_nope, q_pe, ckv, kpe, q_pos, k_pos, w_uk, w_uv):
    B, Sq = q_nope.shape[0], q_nope.shape[1]
    blk = min(Q_BLOCK, Sq)
    nb = Sq // blk
    k_chunk = k_pos // CHUNK

    def to_blocks(t):
        return t.reshape((B, nb, blk) + t.shape[2:]).swapaxes(0, 1)

    def attend(args):
        qn, qp, qpos = args
        q_lat = jnp.einsum('bqhn,rhn->bqhr', qn, w_uk)
        s = jnp.einsum('bqhr,bkr->bhqk', q_lat, ckv) + jnp.einsum('bqhp,bkp->bhqk', qp, kpe)
        s = s.astype(jnp.float32) * SOFTMAX_SCALE
        mask = k_chunk[None, :] <= (qpos // CHUNK)[:, None]
        s = jnp.where(mask, s, NEG_INF)
        pr = jax.nn.softmax(s, axis=-1).astype(ckv.dtype)
        o_lat = jnp.einsum('bhqk,bkr->bqhr', pr, ckv)
        return jnp.einsum('bqhr,rhv->bqhv', o_lat, w_uv)

    out = lax.map(attend, (to_blocks(q_nope), to_blocks(q_pe), q_pos.reshape(nb, blk)))
    return out.swapaxes(0, 1).reshape(B, Sq, MLA_HEADS, V_DIM)


def _layer(x, c, pos, k_pos_past, ckv_past, kpe_past, conv_state, h0, lp):
    B, S, _ = x.shape
    mod = jnp.einsum('bd,de->be', jax.nn.silu(c), lp['ada_w']) + lp['ada_b']
    shift, scale, gate = jnp.split(mod[:, None, :], 3, axis=-1)
    h = _rmsnorm(x, lp['pre_norm']) * (1.0 + scale) + shift
    z = jnp.einsum('bsd,de->bse', h, lp['w_in'])
    xa, ga, cq, ckv, kpe, gb, ua, ub = _split_cols(z)

    conv_in = jnp.concatenate([conv_state.astype(xa.dtype), xa], axis=1)
    xc = lp['conv_b'] + conv_in[:, 0:S] * lp['conv_w'][0]
    for k in range(1, CONV_W):
        xc = xc + conv_in[:, k:k + S] * lp['conv_w'][k]
    new_conv = conv_in[:, S:]
    y_lru, h_last = _rglru(xc, h0, lp['lru_wa'], lp['lru_ba'], lp['lru_wx'], lp['lru_bx'], lp['lru_lambda'])
    ya = y_lru * jax.nn.silu(ga)

    q = jnp.einsum('bsr,re->bse', _rmsnorm(cq, lp['q_norm']), lp['w_q_up'])
    q = q.reshape(B, S, MLA_HEADS, QK_NOPE + QK_ROPE)
    q_nope = q[..., :QK_NOPE]
    q_pe = _rope(q[..., QK_NOPE:], pos)
    ckv = _rmsnorm(ckv, lp['kv_norm'])
    kpe = _rope(kpe[:, :, None, :], pos)[:, :, 0, :]
    if ckv_past is None:
        ckv_all, kpe_all, k_pos = ckv, kpe, pos
    else:
        ckv_all = jnp.concatenate([ckv_past.astype(ckv.dtype), ckv], axis=1)
        kpe_all = jnp.concatenate([kpe_past.astype(kpe.dtype), kpe], axis=1)
        k_pos = jnp.concatenate([k_pos_past, pos])
    attn = _mla_attention(q_nope, q_pe, ckv_all, kpe_all, pos, k_pos, lp['w_uk'], lp['w_uv'])
    yb = attn.reshape(B, S, MLA_WIDTH) * jax.nn.silu(gb)

    merged = (jax.nn.sigmoid(ua) * jnp.einsum('bsw,wd->bsd', ya, lp['w_branch_a'])
              + jax.nn.sigmoid(ub) * jnp.einsum('bsw,wd->bsd', yb, lp['w_branch_b']))
    o = jnp.einsum('bsd,de->bse', merged, lp['w_out'])
    x = x + gate * _rmsnorm(o, lp['post_norm'])
    return x, ckv, kpe, new_conv, h_last


def setup_inputs(seed: int = 0) -> dict:
    key = jax.random.key(seed)
    ks = jax.random.split(key, 32)
    f32 = jnp.float32

    def nrm(k, shape, scale):
        return jax.random.normal(k, shape, f32) * scale

    a0 = jax.random.uniform(ks[31], (DEPTH, LRU_WIDTH), f32, 0.9, 0.999)
    p = a0 ** (1.0 / LRU_C)
    lru_lambda = jnp.log(p) - jnp.log1p(-p)

    return {
        'x_prompt': nrm(ks[0], (BATCH, SEQ, D_MODEL), 1.0),
        'x_sample': nrm(ks[1], (DEC_BATCH, DEC_SEQ, D_MODEL), 1.0),
        'c_prompt': nrm(ks[2], (BATCH, D_MODEL), 1.0),
        'c_sample': nrm(ks[3], (DEC_BATCH, D_MODEL), 1.0),
        'cache_ckv': nrm(ks[4], (DEPTH, DEC_BATCH, PAST_LEN, KV_RANK), 1.0),
        'cache_kpe': nrm(ks[5], (DEPTH, DEC_BATCH, PAST_LEN, QK_ROPE), 1.0),
        'state_conv': nrm(ks[6], (DEPTH, DEC_BATCH, CONV_W - 1, LRU_WIDTH), 1.0),
        'state_lru': nrm(ks[7], (DEPTH, DEC_BATCH, LRU_WIDTH), 0.5),
        'ada_w': nrm(ks[8], (DEPTH, D_MODEL, 3 * D_MODEL), 0.5 * D_MODEL ** -0.5),
        'ada_b': nrm(ks[9], (DEPTH, 3 * D_MODEL), 0.02),
        'pre_norm': 1.0 + nrm(ks[10], (DEPTH, D_MODEL), 0.05),
        'post_norm': 1.0 + nrm(ks[11], (DEPTH, D_MODEL), 0.05),
        'w_in': nrm(ks[12], (DEPTH, D_MODEL, IN_DIM), D_MODEL ** -0.5),
        'conv_w': nrm(ks[13], (DEPTH, CONV_W, LRU_WIDTH), 0.5),
        'conv_b': nrm(ks[14], (DEPTH, LRU_WIDTH), 0.02),
        'lru_wa': nrm(ks[15], (DEPTH, LRU_BLOCKS, LRU_BLOCK_W, LRU_BLOCK_W), LRU_BLOCK_W ** -0.5),
        'lru_ba': nrm(ks[16], (DEPTH, LRU_WIDTH), 0.02),
        'lru_wx': nrm(ks[17], (DEPTH, LRU_BLOCKS, LRU_BLOCK_W, LRU_BLOCK_W), LRU_BLOCK_W ** -0.5),
        'lru_bx': nrm(ks[18], (DEPTH, LRU_WIDTH), 0.02),
        'lru_lambda': lru_lambda,
        'q_norm': 1.0 + nrm(ks[19], (DEPTH, Q_RANK), 0.05),
        'w_q_up': nrm(ks[20], (DEPTH, Q_RANK, MLA_HEADS * (QK_NOPE + QK_ROPE)), Q_RANK ** -0.5),
        'kv_norm': 1.0 + nrm(ks[21], (DEPTH, KV_RANK), 0.05),
        'w_uk': nrm(ks[22], (DEPTH, KV_RANK, MLA_HEADS, QK_NOPE), KV_RANK ** -0.5),
        'w_uv': nrm(ks[23], (DEPTH, KV_RANK, MLA_HEADS, V_DIM), KV_RANK ** -0.5),
        'w_branch_a': nrm(ks[24], (DEPTH, LRU_WIDTH, D_MODEL), LRU_WIDTH ** -0.5),
        'w_branch_b': nrm(ks[25], (DEPTH, MLA_WIDTH, D_MODEL), MLA_WIDTH ** -0.5),
        'w_out': nrm(ks[26], (DEPTH, D_MODEL, D_MODEL), D_MODEL ** -0.5),
    }


def reference(x_prompt, x_sample, c_prompt, c_sample, cache_ckv, cache_kpe, state_conv, state_lru,
              ada_w, ada_b, pre_norm, post_norm, w_in, conv_w, conv_b, lru_wa, lru_ba, lru_wx, lru_bx,
              lru_lambda, q_norm, w_q_up, kv_norm, w_uk, w_uv, w_branch_a, w_branch_b, w_out):
    b_p, s_p = x_prompt.shape[0], x_prompt.shape[1]
    past_len = cache_ckv.shape[2]
    s_s = x_sample.shape[1]
    pos_p = jnp.arange(s_p, dtype=jnp.int32)
    pos_s = past_len + jnp.arange(s_s, dtype=jnp.int32)
    k_pos_past = jnp.arange(past_len, dtype=jnp.int32)
    conv0 = jnp.zeros((b_p, CONV_W - 1, LRU_WIDTH), x_prompt.dtype)
    h0 = jnp.zeros((b_p, LRU_WIDTH), x_prompt.dtype)

    yp, ys = x_prompt, x_sample
    ckv_p, kpe_p, conv_p, lru_p = [], [], [], []
    ckv_s, kpe_s, conv_s, lru_s = [], [], [], []
    for l in range(DEPTH):
        lp = {
            'ada_w': ada_w[l], 'ada_b': ada_b[l], 'pre_norm': pre_norm[l], 'post_norm': post_norm[l],
            'w_in': w_in[l], 'conv_w': conv_w[l], 'conv_b': conv_b[l],
            'lru_wa': lru_wa[l], 'lru_ba': lru_ba[l], 'lru_wx': lru_wx[l], 'lru_bx': lru_bx[l],
            'lru_lambda': lru_lambda[l], 'q_norm': q_norm[l], 'w_q_up': w_q_up[l], 'kv_norm': kv_norm[l],
            'w_uk': w_uk[l], 'w_uv': w_uv[l], 'w_branch_a': w_branch_a[l], 'w_branch_b': w_branch_b[l],
            'w_out': w_out[l],
        }
        yp, a1, a2, a3, a4 = _layer(yp, c_prompt, pos_p, None, None, None, conv0, h0, lp)
        ckv_p.append(a1); kpe_p.append(a2); conv_p.append(a3); lru_p.append(a4)
        ys, b1, b2, b3, b4 = _layer(ys, c_sample, pos_s, k_pos_past, cache_ckv[l], cache_kpe[l],
                                    state_conv[l], state_lru[l], lp)
        ckv_s.append(b1); kpe_s.append(b2); conv_s.append(b3); lru_s.append(b4)

    new_ckv_prompt = jnp.stack(ckv_p)
    new_kpe_prompt = jnp.stack(kpe_p)
    new_conv_prompt = jnp.stack(conv_p)
    new_lru_prompt = jnp.stack(lru_p)
    new_ckv_sample = jnp.stack(ckv_s)
    new_kpe_sample = jnp.stack(kpe_s)
    new_conv_sample = jnp.stack(conv_s)
    new_lru_sample = jnp.stack(lru_s)
    return (yp, ys, new_ckv_prompt, new_kpe_prompt, new_conv_prompt, new_lru_prompt,
            new_ckv_sample, new_kpe_sample, new_conv_sample, new_lru_sample)
```

```python
import numpy as np
import concourse.bass as bass
import concourse.mybir as mybir
from concourse.bass_utils import run_bass_kernel_spmd

F32 = mybir.dt.float32
BF16 = mybir.dt.bfloat16
AF = mybir.ActivationFunctionType
ALU = mybir.AluOpType

EPOCH = 16000


class Buf:
    __slots__ = ("name", "ap", "w", "r")

    def __init__(self, name, ap=None):
        self.name = name
        self.ap = ap
        self.w = {}
        self.r = {}

    def view(self, ap):
        v = Buf(self.name, ap)
        v.w, v.r = self.w, self.r
        return v


class Op:
    __slots__ = ("eng", "fn", "deps", "mark", "cnt", "chan", "val", "idx", "tag")

    def __init__(self, eng, fn, chan=None):
        self.eng = eng
        self.fn = fn
        self.deps = []
        self.mark = False
        self.cnt = None
        self.chan = chan
        self.val = None
        self.idx = None
        self.tag = ""


class Prog:
    ENGS = ("pe", "act", "dve", "pool", "sp")

    def __init__(self, nc):
        self.nc = nc
        self.ops = {e: [] for e in self.ENGS}
        self.chan_cnt = {}
        self.finals = []
        self.nbuf = 0

    def sb(self, name, shape, dt):
        t = self.nc.alloc_sbuf_tensor("sb_" + name, shape, dt)
        return Buf(name, t.ap())

    def ps(self, name, shape, dt=F32):
        t = self.nc.alloc_psum_tensor("ps_" + name, shape, dt)
        return Buf(name, t.ap())

    def _track(self, op, reads, writes):
        key = op.chan if op.chan is not None else op.eng
        isdma = op.chan is not None
        same = key in ("act", "dve", "pool")
        deps = op.deps
        for b in reads:
            for k, o in b.w.items():
                if k != key or same or isdma:
                    deps.append(o)
        for b in writes:
            for k, o in b.w.items():
                if (k != key or same) and o is not op:
                    deps.append(o)
            for k, o in b.r.items():
                if (k != key or same or isdma) and o is not op:
                    deps.append(o)
        for b in reads:
            b.r[key] = op
        for b in writes:
            if isdma:
                b.w.clear()
            else:
                for k in [k for k in b.w if not isinstance(k, tuple)]:
                    del b.w[k]
            b.w[key] = op
            b.r.clear()

    tag = ""

    def op(self, eng, fn, reads=(), writes=()):
        o = Op(eng, fn)
        o.tag = self.tag
        o.idx = len(self.ops[eng])
        self._track(o, reads, writes)
        self.ops[eng].append(o)
        return o

    def dma(self, q, out, in_, reads=(), writes=(), chan=None, final=False, **kw):
        if q == "pool":
            self.nsw = getattr(self, "nsw", 0) + 1
            chan = f"sw{self.nsw}"
        elif chan is None:
            cands = [b for b in list(writes) + list(reads) if b.ap is not None]
            chan = "d_" + cands[0].name
        chan = ("dma", chan)
        o = Op(q, lambda e: e.dma_start(out=out, in_=in_, **kw), chan=chan)
        o.idx = len(self.ops[q])
        self.chan_cnt[chan] = self.chan_cnt.get(chan, 0) + 16
        o.val = self.chan_cnt[chan]
        self._track(o, reads, writes)
        self.ops[q].append(o)
        if final:
            self.finals.append(o)
        return o

    def emit(self):
        nc = self.nc
        for e in self.ENGS:
            for o in self.ops[e]:
                for d in o.deps:
                    if d.chan is None:
                        d.mark = True
        esems = {}
        for e in self.ENGS:
            c = 0
            for o in self.ops[e]:
                if o.chan is None and o.mark:
                    c += 1
                    o.cnt = c
            nep = c // EPOCH + 1
            esems[e] = [nc.alloc_semaphore(f"s_{e}_{i}") for i in range(nep)]
        csems = {ch: nc.alloc_semaphore("c_" + ch[1]) for ch in self.chan_cnt}

        def token(d):
            if d.chan is not None:
                return (csems[d.chan], d.val, d.chan)
            ep, v = divmod(d.cnt - 1, EPOCH)
            return (esems[d.eng][ep], v + 1, (d.eng, ep))

        fin = {}
        for o in self.finals:
            s_, v_, k_ = token(o)
            if k_ not in fin or fin[k_][1] < v_:
                fin[k_] = (s_, v_, k_)
        finals = list(fin.values())
        nwaits = [0]

        def run(e, eng):
            seen = {}
            lst = self.ops[e]
            for o in lst:
                need = {}
                for d in o.deps:
                    s, v, k = token(d)
                    if seen.get(k, 0) >= v:
                        continue
                    if k not in need or need[k][1] < v:
                        need[k] = (s, v)
                for k, (s, v) in need.items():
                    eng.wait_ge(s, v)
                    seen[k] = v
                    nwaits[0] += 1
                ins = o.fn(eng)
                if o.chan is not None:
                    ins.then_inc(csems[o.chan], 16)
                elif o.mark:
                    ep = (o.cnt - 1) // EPOCH
                    ins.then_inc(esems[e][ep], 1)
            if e == "sp":
                for s, v, k in finals:
                    eng.wait_ge(s, v)

        with nc.Block() as block:
            @block.tensor
            def _(eng):
                run("pe", eng)

            @block.scalar
            def _(eng):
                run("act", eng)

            @block.vector
            def _(eng):
                run("dve", eng)

            @block.gpsimd
            def _(eng):
                run("pool", eng)

            @block.sync
            def _(eng):
                run("sp", eng)
        self.stats = {e: len(self.ops[e]) for e in self.ENGS}
        self.stats["waits"] = nwaits[0]
        self.stats["sems"] = sum(len(v) for v in esems.values()) + len(csems)


D = 1024
CQ = 768
OFF = dict(xa=0, ga=1024, cq=2048, ckv=2816, kpe=3072, gb=3136, ua=4160, ub=5184)
IN_DIM = 6208
SCALE = float(192 ** -0.5)
EPS = 1e-6
NEG = -30000.0
PAST = 2048
FAST_RCP = False


class MK:
    def __init__(self, SEQ=4096, NPS=2, do_sample=True, stop=None):
        self.SEQ, self.NPS, self.do_sample = SEQ, NPS, do_sample
        self.stop = stop
        nc = self.nc = bass.Bass("TRN2", target_bir_lowering=False)
        p = self.p = Prog(nc)
        I = lambda n, s, dt=F32: nc.dram_tensor(n, list(s), dt, kind="ExternalInput").ap()
        Oo = lambda n, s, dt=F32: nc.dram_tensor(n, list(s), dt, kind="ExternalOutput").ap()
        N = lambda n, s, dt=F32: nc.dram_tensor(n, list(s), dt, kind="Internal").ap()
        d = self.d = {}
        for n, s in [("xp", (NPS, SEQ, D)), ("xs", (64, D)), ("cp", (NPS, D)), ("cs", (4, D)),
                     ("cckv", (2, 4, PAST, 256)), ("ckpe", (2, 4, PAST, 64)), ("sconv", (2, 4, 3, D)),
                     ("slru", (2, 4, D)), ("ada_w", (2, D, 3 * D)), ("ada_b", (2, 3 * D)),
                     ("pre_norm", (2, D)), ("post_norm", (2, D)), ("w_in", (2, D, IN_DIM)),
                     ("conv_w", (2, 4, D)), ("conv_b", (2, D)), ("lru_wa", (2, 8, 128, 128)),
                     ("lru_ba", (2, D)), ("lru_wx", (2, 8, 128, 128)), ("lru_bx", (2, D)),
                     ("lru_lambda", (2, D)), ("q_norm", (2, CQ)), ("w_q_up", (2, CQ, 1536)),
                     ("kv_norm", (2, 256)), ("w_uk", (2, 256, 1024)), ("w_uv", (2, 256, 1024)),
                     ("w_branch_a", (2, D, D)), ("w_branch_b", (2, D, D)), ("w_out", (2, D, D)),
                     ("ident", (128, 128)), ("cosT", (128, 4096)), ("ssinT", (128, 4096)),
                     ("cos_tm", (4096, 64)), ("ssin_tm", (4096, 64)),
                     ("cosT_s", (128, 64)), ("ssinT_s", (128, 64)), ("cos_tm_s", (64, 64)),
                     ("ssin_tm_s", (64, 64)), ("sel_s", (4, 64)), ("mrow", (1, 1024))]:
            d[n] = I(n, s)
        for n, s in [("yp", (NPS, SEQ, D)), ("ys", (64, D)), ("nckv_p", (2, NPS, SEQ, 256)),
                     ("nkpe_p", (2, NPS, SEQ, 64)), ("nconv_p", (2, NPS, 3, D)), ("nlru_p", (2, NPS, D)),
                     ("nckv_s", (2, 64, 256)), ("nkpe_s", (2, 64, 64)), ("nconv_s", (2, 4, 3, D)),
                     ("nlru_s", (2, 4, D))]:
            d[n] = Oo(n, s)
        d["x1p"] = N("x1p", (NPS, SEQ, D))
        d["x1s"] = N("x1s", (64, D))
        d["Win"] = N("Win", (2, D, IN_DIM), BF16)
        d["Wq"] = N("Wq", (2, CQ, 2112), BF16)
        d["Wa"] = N("Wa", (2, D, D), BF16)
        d["Wb"] = N("Wb", (2, D, D), BF16)
        d["Wo"] = N("Wo", (2, D, D), BF16)
        self.dbuf = {}
        self.alloc()
        try:
            self.setup()
            self.chk("setup")
            for s in range(NPS):
                for l in range(2):
                    self.run_pass(l, "p", s)
            if do_sample:
                for l in range(2):
                    self.run_pass(l, "s", 0)
        except StopIteration:
            pass
        p.emit()

    def chk(self, name):
        if self.stop == name:
            raise StopIteration

    def DB(self, name):
        if name not in self.dbuf:
            self.dbuf[name] = Buf(name)
        return self.dbuf[name]

    def alloc(self):
        p = self.p
        sb = p.sb
        self.ident_f = sb("ident_f", [128, 128], F32)
        self.ident_b = sb("ident_b", [128, 128], BF16)
        self.ones_b = sb("ones_b", [128, 128], BF16)
        self.ones_f = sb("ones_f", [1, 128], F32)
        self.ones_ff = sb("ones_ff", [128, 128], F32)
        self.accp = [sb(f"accp{i}", [128, 512], F32) for i in range(2)]
        self.mrow_f = sb("mrow_f", [1, 1024], F32)
        self.mrow = sb("mrow", [1, 1024], BF16)
        self.sel_s = sb("sel_s", [4, 64], F32)
        ckvT = sb("ckvT", [128, 2, 4096], BF16)
        ckvtm = sb("ckvtm", [128, 32, 256], BF16)
        kpeT = sb("kpeT", [128, 4096], BF16)
        self.K_T = [Buf(f"ckvT{k}", ckvT.ap[:, :, k * 128:(k + 1) * 128]) for k in range(32)]
        self.K_tm = [Buf(f"ckvtm{k}", ckvtm.ap[:, k, :]) for k in range(32)]
        self.K_pe = [Buf(f"kpeT{k}", kpeT.ap[:, k * 128:(k + 1) * 128]) for k in range(32)]
        self.ckvT_all, self.ckvtm_all, self.kpeT_all = ckvT, ckvtm, kpeT
        self.xt = [sb(f"xt{i}", [128, 1024], F32) for i in range(2)]
        self.xn = [sb(f"xn{i}", [128, 1024], BF16) for i in range(2)]
        self.hT = sb("hT", [128, 8, 512], BF16)
        self.hTc = [Buf(f"hT{c}", self.hT.ap[:, c, :]) for c in range(8)]
        self.ring = [sb(f"ring{i}", [128, 4096], BF16) for i in range(3)]
        self.ring_i = 0
        cq = sb("cq", [128, 6, 512], BF16)
        self.cq = [Buf(f"cq{e}", cq.ap[:, e, :]) for e in range(6)]
        qlat = sb("qlat", [128, 2, 2, 512], BF16)
        self.qlat_p = [Buf(f"qlatp{i}", qlat.ap[:, i]) for i in range(2)]
        qls = sb("qlat_s", [128, 8, 2, 64], BF16)
        self.qlat_s = [Buf(f"qlats{i}", qls.ap[:, i]) for i in range(8)]
        qpe = sb("qpe", [128, 8, 512], BF16)
        self.qpe = [Buf(f"qpe{j}", qpe.ap[:, j, :]) for j in range(8)]
        self.maskt = sb("maskt", [128, 640], BF16)
        self.cosT = sb("cosT_t", [128, 512], F32)
        self.ssinT = sb("ssinT_t", [128, 512], F32)
        self.cos_tm = sb("cos_tm_t", [128, 4, 64], F32)
        self.ssin_tm = sb("ssin_tm_t", [128, 4, 64], F32)
        ya = sb("yaT", [128, 8, 512], BF16)
        yb = sb("ybT", [128, 8, 512], BF16)
        mg = sb("mgT", [128, 8, 512], BF16)
        self.ya = [Buf(f"ya{c}", ya.ap[:, c, :]) for c in range(8)]
        self.yb = [Buf(f"yb{c}", yb.ap[:, c, :]) for c in range(8)]
        self.mg = [Buf(f"mg{c}", mg.ap[:, c, :]) for c in range(8)]
        self.GG = sb("GG", [128, 1024], F32)
        self.ckv_st = [sb(f"ckv_st{i}", [128, 256], F32) for i in range(2)]
        self.kpe_st = [sb(f"kpe_st{i}", [128, 64], F32) for i in range(2)]
        self.kpedup = sb("kpedup", [128, 128], BF16)
        self.wukT = sb("wukT", [128, 8, 256], BF16)
        self.wuv = sb("wuv", [128, 2, 1024], BF16)
        self.wa = sb("wa", [128, 8, 128], BF16)
        self.wx = sb("wx", [128, 8, 128], BF16)
        self.diag = sb("diag", [128, 8, 4, 128], BF16)
        self.cw = sb("cw", [128, 8, 4], F32)
        self.vec = {n: sb("v_" + n, [128, 8], F32) for n in
                    ("conv_b", "lru_ba", "lru_bx", "lru_lambda", "pre_norm", "post_norm", "hba", "hbx",
                     "sp", "sp8", "hsp")}
        self.qg = sb("qg", [128, 6], F32)
        self.kvg = sb("kvg", [128, 256], F32)
        self.adab = sb("adab", [128, 24], F32)
        self.modT = sb("modT", [128, 24, 4], F32)
        self.gs = sb("gs", [128, 8, 4], F32)
        self.ggT = sb("ggT", [128, 8, 4], F32)
        self.gg_tm = self.c_sb = self.xt[0]
        self.c_t = self.xt[1]
        self.eps = sb("eps", [128, 1], F32)
        self.one = sb("one", [128, 1], F32)
        self.scT = sb("scT", [128, 8, 4], F32)
        self.hcar = sb("hcar", [128, 8, 4], F32)
        self.carry = sb("carry", [128, 8, 4, 4], BF16)
        self.carry_f = sb("carry_f", [128, 8, 4, 3], F32)
        self.convlast = sb("convlast", [128, 8, 4, 3], F32)
        self.stat = [sb(f"stat{i}", [128, 4], F32) for i in range(4)]
        self.stat_i = 0
        self.xab = sb("xab", [128, 516], BF16)
        self.sgb_s = sb("sgb_s", [128, 8, 64], BF16)
        self.olat = sb("olat", [128, 2, 512], BF16)
        NF, NB = 7, 6
        self.ppool = [sb(f"pp{i}", [128, 512], BF16) for i in range(3)]
        self.pi = 0
        self.sgbp = [sb(f"sgbp{i}", [128, 512], BF16) for i in range(2)]
        self.fpool = [sb(f"fp{i}", [128, 512], F32) for i in range(NF)]
        self.bpool = [sb(f"bp{i}", [128, 512], BF16) for i in range(NB)]
        self.fi = self.bi = 0
        ps = p.ps
        self.S = [ps(f"S{i}", [128, 512]) for i in range(2)]
        self.O = [ps(f"O{i}", [128, 512]) for i in range(3)]
        self.G = [ps(f"G{i}", [128, 512]) for i in range(3)]
        self.TB = [b.view(b.ap.bitcast(BF16)) for b in self.G]
        self.gi = self.ti = self.si = 0

    def tf(self):
        self.fi = (self.fi + 1) % len(self.fpool)
        return self.fpool[self.fi]

    def tb(self):
        self.bi = (self.bi + 1) % len(self.bpool)
        return self.bpool[self.bi]

    def gb(self):
        self.gi = (self.gi + 1) % len(self.G)
        return self.G[self.gi]

    def tp(self):
        self.pi = (self.pi + 1) % len(self.ppool)
        return self.ppool[self.pi]

    def tbk(self):
        self.gi = (self.gi + 1) % len(self.G)
        return self.TB[self.gi]

    def st(self):
        self.stat_i = (self.stat_i + 1) % len(self.stat)
        return self.stat[self.stat_i]

    def rslot(self):
        self.ring_i = (self.ring_i + 1) % len(self.ring)
        return self.ring[self.ring_i]

    def mm(self, out, lhsT, rhs, start, stop, R, W):
        self.p.op("pe", lambda e: e.matmul(out, lhsT=lhsT, rhs=rhs, start=start, stop=stop), R, W)

    def tr(self, out, in_, ident, R, W):
        self.p.op("pe", lambda e: e.transpose(out=out, in_=in_, identity=ident), R, W)

    def act(self, out, in_, func, R, W, scale=1.0, bias=0.0, accum=None):
        if accum is None:
            self.p.op("act", lambda e: e.activation(out=out, in_=in_, func=func, bias=bias, scale=scale), R, W)
        else:
            self.p.op("act", lambda e: e.activation(out=out, in_=in_, func=func, bias=bias, scale=scale,
                                                    accum_out=accum), R, W)

    def ts(self, out, in0, s1, s2, op0, op1, R, W, eng="dve"):
        if s2 is None:
            self.p.op(eng, lambda e: e.tensor_scalar(out=out, in0=in0, scalar1=s1, scalar2=None, op0=op0), R, W)
        else:
            self.p.op(eng, lambda e: e.tensor_scalar(out=out, in0=in0, scalar1=s1, scalar2=s2, op0=op0, op1=op1), R, W)

    def stt(self, out, in0, sc, in1, op0, op1, R, W):
        self.p.op("dve", lambda e: e.scalar_tensor_tensor(out=out, in0=in0, scalar=sc, in1=in1, op0=op0, op1=op1), R, W)

    def tt(self, out, in0, in1, op, R, W, eng="dve"):
        self.p.op(eng, lambda e: e.tensor_tensor(out=out, in0=in0, in1=in1, op=op), R, W)

    def cp(self, out, in_, R, W, eng="dve"):
        self.p.op(eng, lambda e: e.tensor_copy(out=out, in_=in_), R, W)

    def rcp(self, out, in_, R, W):
        self.p.op("dve", lambda e: e.reciprocal(out=out, in_=in_), R, W)

    def ld(self, out, in_, R, W, q="sp", **kw):
        return self.p.dma(q, out, in_, reads=R, writes=W, **kw)

    def rstd(self, ssq, n, rows):
        pass

    def setup(self):
        d, p = self.d, self.p
        self.ld(self.ident_f.ap, d["ident"], [], [self.ident_f])
        self.cp(self.ident_b.ap, self.ident_f.ap, [self.ident_f], [self.ident_b])
        self.p.op("dve", lambda e: e.memset(self.ones_b.ap, 1.0), [], [self.ones_b])
        self.p.op("dve", lambda e: e.memset(self.ones_f.ap, 1.0), [], [self.ones_f])
        self.p.op("dve", lambda e: e.memset(self.ones_ff.ap, 1.0), [], [self.ones_ff])
        self.p.op("dve", lambda e: e.memset(self.eps.ap, EPS), [], [self.eps])
        self.p.op("dve", lambda e: e.memset(self.one.ap, 1.0), [], [self.one])
        self.ld(self.mrow_f.ap, d["mrow"], [], [self.mrow_f])
        self.cp(self.mrow.ap, self.mrow_f.ap, [self.mrow_f], [self.mrow])
        self.ld(self.sel_s.ap, d["sel_s"], [], [self.sel_s])
        self.p.op("dve", lambda e: e.memset(self.maskt.ap, 0.0), [], [self.maskt])
        self.cp(self.maskt.ap[0:1, :], self.mrow.ap[0:1, 0:640], [self.mrow], [self.maskt])
        self.p.op("dve", lambda e: e.memset(self.kpedup.ap, 0.0), [], [self.kpedup])
        for l in range(2):
            for name, src in (("Win", "w_in"), ("Wa", "w_branch_a"), ("Wb", "w_branch_b"), ("Wo", "w_out")):
                self.ld(d[name][l].rearrange("k (a b) -> k a b", a=4), d[src][l].rearrange("k (a b) -> k a b", a=4),
                        [], [self.DB(f"{name}{l}"), self.DB("castchain")], q="pool", chan=f"cast_{name}{l}")
            src = d["w_q_up"][l].rearrange("k (h e) -> k h e", h=8)
            dst = d["Wq"][l]
            B = self.DB(f"Wq{l}")
            kw = dict(q="pool")
            B = self.DB(f"Wq{l}")
            self.ld(dst[:, 0:1024].rearrange("k (h e) -> k h e", h=8), src[:, :, 0:128], [], [B, self.DB("castchain")], chan=f"cast_Wq{l}a", **kw)
            self.ld(dst[:, 1024:1536].rearrange("k (h e) -> k h e", h=8), src[:, :, 128:192], [], [B, self.DB("castchain")], chan=f"cast_Wq{l}b", **kw)
            sw = dst[:, 1536:2048].rearrange("k (h e) -> k h e", h=8)
            self.ld(sw[:, :, 0:32], src[:, :, 160:192], [], [B, self.DB("castchain")], chan=f"cast_Wq{l}c", **kw)
            self.ld(sw[:, :, 32:64], src[:, :, 128:160], [], [B, self.DB("castchain")], chan=f"cast_Wq{l}d", **kw)
            self.ld(dst[:, 2048:2112], d["w_q_up"][l][:, 128:192], [], [B, self.DB("castchain")], chan=f"cast_Wq{l}e", **kw)

    def wload(self, name, l, rows, c0, ncols):
        slot = self.rslot()
        kc = rows // 128
        view = slot.ap[:, 0:kc * ncols].rearrange("p (k e) -> p k e", k=kc)
        src = self.d[name][l].rearrange("(k p) e -> p k e", p=128)[:, :, c0:c0 + ncols]
        self.ld(view, src, [self.DB(f"{name}{l}")], [slot])
        return slot, view

    def run_pass(self, l, kind, seq):
        d, p, v = self.d, self.p, self.vec
        g = self.g = type("G", (), {})()
        g.l, g.kind, g.seq = l, kind, seq
        if kind == "p":
            g.T, g.nt, g.nseg, g.L, g.nsub, g.tsz = 512, self.SEQ // 512, 1, 512, 4, 128
            g.c_src = d["cp"][seq:seq + 1, :]
            g.sel = self.ones_f.ap[0:1, 0:128]
        else:
            g.T, g.nt, g.nseg, g.L, g.nsub, g.tsz = 64, 1, 4, 16, 1, 64
            g.c_src = d["cs"]
            g.sel = self.sel_s.ap
        ns = g.nseg
        NC = dict(allow_slow_non_contiguous=True)
        self.p.tag = "PASS"
        self.ld(self.wuv.ap, d["w_uv"][l].rearrange("(rc p) e -> p rc e", p=128), [], [self.wuv], q="pool")
        self.ld(self.wa.ap, d["lru_wa"][l].rearrange("n k j -> k n j"), [], [self.wa], q="pool")
        self.ld(self.wx.ap, d["lru_wx"][l].rearrange("n k j -> k n j"), [], [self.wx], q="pool")
        slot = self.rslot()
        wukf = slot.ap.bitcast(F32).rearrange("p (rc e) -> p rc e", rc=2)
        self.ld(wukf, d["w_uk"][l].rearrange("(rc p) e -> p rc e", p=128), [], [slot])
        for h in range(8):
            G = self.gb()
            for rc in range(2):
                self.tr(G.ap[:, rc * 128:(rc + 1) * 128], wukf[:, rc, h * 128:(h + 1) * 128], self.ident_f.ap, [slot, self.ident_f], [G])
            self.cp(self.wukT.ap[:, h, :], G.ap[:, 0:256], [G], [self.wukT])
        for k in range(4):
            self.ld(self.cw.ap[:, :, k], d["conv_w"][l, k].rearrange("(c p) -> p c", p=128), [], [self.cw], **NC)
        for n in ("conv_b", "lru_ba", "lru_bx", "lru_lambda", "pre_norm", "post_norm"):
            self.ld(v[n].ap, d[n][l].rearrange("(c p) -> p c", p=128), [], [v[n]], **NC)
        self.ld(self.qg.ap, d["q_norm"][l].rearrange("(c p) -> p c", p=128), [], [self.qg], **NC)
        self.ld(self.adab.ap, d["ada_b"][l].rearrange("(c p) -> p c", p=128), [], [self.adab], **NC)
        self.ld(self.kvg.ap, d["kv_norm"][l].partition_broadcast(128), [], [self.kvg])
        self.ts(v["hba"].ap, v["lru_ba"].ap, 0.5, None, ALU.mult, None, [v["lru_ba"]], [v["hba"]])
        self.ts(v["hbx"].ap, v["lru_bx"].ap, 0.5, None, ALU.mult, None, [v["lru_bx"]], [v["hbx"]])
        self.act(v["sp"].ap, v["lru_lambda"].ap, AF.Exp, [v["lru_lambda"]], [v["sp"]], scale=-1.0)
        self.act(v["sp"].ap, v["sp"].ap, AF.Ln, [v["sp"]], [v["sp"]], bias=1.0)
        self.ts(v["sp8"].ap, v["sp"].ap, -8.0, None, ALU.mult, None, [v["sp"]], [v["sp8"]])
        self.ts(v["hsp"].ap, v["sp"].ap, -4.0, None, ALU.mult, None, [v["sp"]], [v["hsp"]])
        for c in range(8):
            for k in range(4):
                self.ts(self.diag.ap[:, c, k, :], self.ident_f.ap, self.cw.ap[:, c, k:k + 1], None, ALU.mult, None,
                        [self.ident_f, self.cw], [self.diag])
        self.ld(self.c_sb.ap[0:ns, :], g.c_src, [], [self.c_sb])
        self.act(self.c_t.ap[0:ns, :], self.c_sb.ap[0:ns, :], AF.Tanh, [self.c_sb], [self.c_t], scale=0.5)
        self.ts(self.c_t.ap[0:ns, :], self.c_t.ap[0:ns, :], 0.5, 0.5, ALU.mult, ALU.add, [self.c_t], [self.c_t])
        self.tt(self.c_t.ap[0:ns, :], self.c_t.ap[0:ns, :], self.c_sb.ap[0:ns, :], ALU.mult, [self.c_t, self.c_sb], [self.c_t])
        for c in range(8):
            G = self.gb()
            self.tr(G.ap[:, 0:ns], self.c_t.ap[0:ns, c * 128:(c + 1) * 128], self.ident_f.ap[0:ns, 0:ns], [self.c_t, self.ident_f], [G])
            self.cp(self.scT.ap[:, c, 0:ns], G.ap[:, 0:ns], [G], [self.scT])
        for e in range(24):
            slot = self.rslot()
            wv = slot.ap.bitcast(F32)[:, 0:1024].rearrange("p (k e) -> p k e", k=8)
            self.ld(wv, d["ada_w"][l].rearrange("(k p) e -> p k e", p=128)[:, :, e * 128:(e + 1) * 128], [], [slot])
            G = self.gb()
            for kc in range(8):
                self.mm(G.ap[:, 0:ns], wv[:, kc, :], self.scT.ap[:, kc, 0:ns], kc == 0, kc == 7, [slot, self.scT], [G])
            self.ts(self.modT.ap[:, e, 0:ns], G.ap[:, 0:ns], self.adab.ap[:, e:e + 1], None, ALU.add, None, [G, self.adab], [self.modT])
        for c in range(8):
            self.ts(self.gs.ap[:, c, 0:ns], self.modT.ap[:, 8 + c, 0:ns], 1.0, v["pre_norm"].ap[:, c:c + 1], ALU.add, ALU.mult,
                    [self.modT, v["pre_norm"]], [self.gs])
            self.ts(self.ggT.ap[:, c, 0:ns], self.modT.ap[:, 16 + c, 0:ns], v["post_norm"].ap[:, c:c + 1], None, ALU.mult, None,
                    [self.modT, v["post_norm"]], [self.ggT])
        for half in range(2):
            G = self.gb()
            for cc in range(4):
                c = half * 4 + cc
                self.tr(G.ap[0:ns, cc * 128:(cc + 1) * 128], self.ggT.ap[:, c, 0:ns], self.ident_f.ap, [self.ggT, self.ident_f], [G])
            self.cp(self.gg_tm.ap[0:ns, half * 512:(half + 1) * 512], G.ap[0:ns, :], [G], [self.gg_tm])
        for half in range(2):
            G = self.gb()
            self.mm(G.ap[0:g.tsz, :], g.sel, self.gg_tm.ap[0:ns, half * 512:(half + 1) * 512], True, True, [self.gg_tm, self.sel_s, self.ones_f], [G])
            self.cp(self.GG.ap[0:g.tsz, half * 512:(half + 1) * 512], G.ap[0:g.tsz, :], [G], [self.GG])
        if kind == "p":
            self.p.op("dve", lambda e: e.memset(self.hcar.ap, 0.0), [], [self.hcar])
            self.p.op("dve", lambda e: e.memset(self.carry.ap, 0.0), [], [self.carry])
        else:
            for g_ in range(4):
                self.ld(self.hcar.ap[:, :, g_], d["slru"][l, g_].rearrange("(c p) -> p c", p=128), [], [self.hcar], **NC)
                for k in range(3):
                    self.ld(self.carry_f.ap[:, :, g_, k], d["sconv"][l, g_, k].rearrange("(c p) -> p c", p=128), [], [self.carry_f], **NC)
            self.cp(self.carry.ap[:, :, :, 0:3], self.carry_f.ap, [self.carry_f], [self.carry])
        self.chk("pass_setup")
        for t in range(g.nt):
            self.tile(t)
            self.chk("tile")
        if kind == "p":
            for k in range(3):
                self.ld(d["nconv_p"][l, seq, k].rearrange("(c p) -> p c", p=128), self.convlast.ap[:, :, 0, k],
                        [self.convlast], [self.DB("o_nconv")], q="act", final=True, **NC)
            self.ld(d["nlru_p"][l, seq].rearrange("(c p) -> p c", p=128), self.hcar.ap[:, :, 0],
                    [self.hcar], [self.DB("o_nlru")], q="act", final=True, **NC)
        else:
            for g_ in range(4):
                for k in range(3):
                    self.ld(d["nconv_s"][l, g_, k].rearrange("(c p) -> p c", p=128), self.convlast.ap[:, :, g_, k],
                            [self.convlast], [self.DB("o_nconv")], q="act", final=True, **NC)
                self.ld(d["nlru_s"][l, g_].rearrange("(c p) -> p c", p=128), self.hcar.ap[:, :, g_],
                        [self.hcar], [self.DB("o_nlru")], q="act", final=True, **NC)

    def stageA_gen(self, t):
        g, d = self.g, self.d
        l, T, ns, L, nsub, tsz, seq = g.l, g.T, g.nseg, g.L, g.nsub, g.tsz, g.seq
        prompt = g.kind == "p"
        if prompt:
            xsrc = (d["xp"] if l == 0 else d["x1p"])[seq]
            row0 = t * 512
            xn_ = lambda s: f"x1p_{seq}_{t}_{s}"
        else:
            xsrc = d["xs"] if l == 0 else d["x1s"]
            row0 = 0
            xn_ = lambda s: "x1s"
        hT, hTc = self.hT, self.hTc
        self.p.tag = "A"
        if prompt:
            self.ld(self.cosT.ap, d["cosT"][:, row0:row0 + 512], [], [self.cosT])
            self.ld(self.ssinT.ap, d["ssinT"][:, row0:row0 + 512], [], [self.ssinT])
            self.ld(self.cos_tm.ap, d["cos_tm"][row0:row0 + 512, :].rearrange("(s p) e -> p s e", p=128), [], [self.cos_tm])
            self.ld(self.ssin_tm.ap, d["ssin_tm"][row0:row0 + 512, :].rearrange("(s p) e -> p s e", p=128), [], [self.ssin_tm])
        else:
            self.ld(self.cosT.ap[:, 0:64], d["cosT_s"], [], [self.cosT])
            self.ld(self.ssinT.ap[:, 0:64], d["ssinT_s"], [], [self.ssinT])
            self.ld(self.cos_tm.ap[0:64, 0, :], d["cos_tm_s"], [], [self.cos_tm])
            self.ld(self.ssin_tm.ap[0:64, 0, :], d["ssin_tm_s"], [], [self.ssin_tm])
        slot, wkv = self.wload("Win", l, 1024, OFF["ckv"], 320)
        for s in range(nsub):
            xt, xn = self.xt[s % 2], self.xn[s % 2]
            r0 = row0 + s * 128
            self.ld(xt.ap[0:tsz, :], xsrc[r0:r0 + tsz, :], [self.DB(xn_(s))] if l == 1 else [], [xt])
            st = self.st()
            self.act(xn.ap[0:tsz, :], xt.ap[0:tsz, :], AF.Square, [xt], [xn, st], accum=st.ap[0:tsz, 0:1])
            self.chk("A1")
            self.act(st.ap[0:tsz, 1:2], st.ap[0:tsz, 0:1], AF.Sqrt, [st], [st], scale=1.0 / D, bias=self.eps.ap[0:tsz, :])
            self.chk("A2")
            self.rcp(st.ap[0:tsz, 2:3], st.ap[0:tsz, 1:2], [st], [st])
            self.act(xn.ap[0:tsz, :], xt.ap[0:tsz, :], AF.Identity, [xt, st], [xn], scale=st.ap[0:tsz, 2:3])
            self.chk("A3")
            for half in range(2):
                TBk = self.tbk()
                for cc in range(4):
                    c = half * 4 + cc
                    self.tr(TBk.ap[:, cc * 128:cc * 128 + tsz], xn.ap[0:tsz, c * 128:(c + 1) * 128],
                            self.ident_b.ap[0:tsz, 0:tsz], [xn, self.ident_b], [TBk])
                self.chk("A4")
                for cc in range(4):
                    c = half * 4 + cc
                    for sg in range(ns):
                        c0, n = (0, tsz) if prompt else (sg * L, L)
                        self.act(hT.ap[:, c, s * 128 + c0:s * 128 + c0 + n], TBk.ap[:, cc * 128 + c0:cc * 128 + c0 + n],
                                 AF.Identity, [TBk, self.gs, self.modT], [hTc[c]],
                                 scale=self.gs.ap[:, c, sg:sg + 1], bias=self.modT.ap[:, c, sg:sg + 1])
            self.p.tag = "B1"
            G = self.gb()
            for kc in range(8):
                self.mm(G.ap[0:tsz, 0:320], hT.ap[:, kc, s * 128:s * 128 + tsz], wkv[:, kc, :], kc == 0, kc == 7, [hTc[kc], slot], [G])
            st, junk = self.st(), self.tb()
            self.act(junk.ap[0:tsz, 0:256], G.ap[0:tsz, 0:256], AF.Square, [G], [junk, st], accum=st.ap[0:tsz, 0:1])
            self.act(st.ap[0:tsz, 1:2], st.ap[0:tsz, 0:1], AF.Sqrt, [st], [st], scale=1.0 / 256, bias=self.eps.ap[0:tsz, :])
            self.rcp(st.ap[0:tsz, 2:3], st.ap[0:tsz, 1:2], [st], [st])
            cst, kst = self.ckv_st[s % 2], self.kpe_st[s % 2]
            self.stt(cst.ap[0:tsz, :], G.ap[0:tsz, 0:256], st.ap[0:tsz, 2:3], self.kvg.ap[0:tsz, :], ALU.mult, ALU.mult, [G, st, self.kvg], [cst])
            kt = (t * 4 + s) if prompt else 16
            KM, KT, KP = self.K_tm[kt], self.K_T[kt], self.K_pe[kt]
            self.act(KM.ap[0:tsz, :], cst.ap[0:tsz, :], AF.Copy, [cst], [KM])
            r0 = row0 + s * 128
            if prompt:
                self.ld(d["nckv_p"][l, seq, r0:r0 + tsz, :], cst.ap[0:tsz, :], [cst], [self.DB("o_ckv")], q="act", final=True)
            else:
                self.ld(d["nckv_s"][l], cst.ap[0:tsz, :], [cst], [self.DB("o_ckv")], q="act", final=True)
            t1 = self.tf()
            cosm, sinm = self.cos_tm.ap[0:tsz, s, :], self.ssin_tm.ap[0:tsz, s, :]
            self.tt(kst.ap[0:tsz, :], G.ap[0:tsz, 256:320], cosm, ALU.mult, [G, self.cos_tm], [kst])
            self.tt(t1.ap[0:tsz, 0:32], G.ap[0:tsz, 288:320], sinm[:, 0:32], ALU.mult, [G, self.ssin_tm], [t1])
            self.tt(t1.ap[0:tsz, 32:64], G.ap[0:tsz, 256:288], sinm[:, 32:64], ALU.mult, [G, self.ssin_tm], [t1])
            self.tt(kst.ap[0:tsz, :], kst.ap[0:tsz, :], t1.ap[0:tsz, 0:64], ALU.add, [kst, t1], [kst])
            if prompt:
                self.ld(d["nkpe_p"][l, seq, r0:r0 + tsz, :], kst.ap[0:tsz, :], [kst], [self.DB("o_kpe")], q="act", final=True)
            else:
                self.ld(d["nkpe_s"][l], kst.ap[0:tsz, :], [kst], [self.DB("o_kpe")], q="act", final=True)
            self.act(self.kpedup.ap[0:tsz, 0:64], kst.ap[0:tsz, :], AF.Copy, [kst], [self.kpedup])
            TBk = self.tbk()
            idb = self.ident_b.ap[0:tsz, 0:tsz]
            self.tr(TBk.ap[:, 0:tsz], KM.ap[0:tsz, 0:128], idb, [KM, self.ident_b], [TBk])
            self.tr(TBk.ap[:, 128:128 + tsz], KM.ap[0:tsz, 128:256], idb, [KM, self.ident_b], [TBk])
            self.tr(TBk.ap[:, 256:256 + tsz], self.kpedup.ap[0:tsz, :], idb, [self.kpedup, self.ident_b], [TBk])
            self.act(KT.ap[:, 0, 0:tsz], TBk.ap[:, 0:tsz], AF.Copy, [TBk], [KT])
            self.act(KT.ap[:, 1, 0:tsz], TBk.ap[:, 128:128 + tsz], AF.Copy, [TBk], [KT])
            self.act(KP.ap[:, 0:tsz], TBk.ap[:, 256:256 + tsz], AF.Copy, [TBk], [KP])
            yield
            self.p.tag = "A"

    def tile(self, t):
        g, d, v = self.g, self.d, self.vec
        l, T, ns, L, nsub, tsz, seq = g.l, g.T, g.nseg, g.L, g.nsub, g.tsz, g.seq
        prompt = g.kind == "p"
        last_tile = t == g.nt - 1
        if prompt:
            xsrc = (d["xp"] if l == 0 else d["x1p"])[seq]
            ydst = (d["x1p"] if l == 0 else d["yp"])[seq]
            row0 = t * 512
            xn_ = lambda s: f"x1p_{seq}_{t}_{s}"
        else:
            xsrc = d["xs"] if l == 0 else d["x1s"]
            ydst = d["x1s"] if l == 0 else d["ys"]
            row0 = 0
            xn_ = lambda s: "x1s"
        hT, hTc = self.hT, self.hTc
        self.chk("A0")
        self.p.tag = "A"
        if getattr(self, "stageA_done", None) != (id(g), t):
            for _ in self.stageA_gen(t):
                pass
        self.chk("A")
        self.chk("B1")
        self.p.tag = "B2"
        sA, wA = self.wload("Win", l, 1024, OFF["cq"], 384)
        sB, wB = self.wload("Win", l, 1024, OFF["cq"] + 384, 384)
        cq = self.cq
        SS = self.O[2]
        for e in range(6):
            sl, w = (sA, wA) if e < 3 else (sB, wB)
            G = self.gb()
            for kc in range(8):
                self.mm(G.ap[:, 0:T], w[:, kc, (e % 3) * 128:(e % 3 + 1) * 128], hT.ap[:, kc, 0:T], kc == 0, kc == 7, [sl, hTc[kc]], [G])
            sq = self.tb()
            self.act(sq.ap[:, 0:T], G.ap[:, 0:T], AF.Square, [G], [sq])
            self.act(cq[e].ap[:, 0:T], G.ap[:, 0:T], AF.Copy, [G], [cq[e]])
            self.mm(SS.ap[:, 0:T], self.ones_b.ap, sq.ap[:, 0:T], e == 0, e == 5, [sq, self.ones_b], [SS])
        rq = self.tf()
        self.act(rq.ap[:, 0:T], SS.ap[:, 0:T], AF.Sqrt, [SS], [rq], scale=1.0 / CQ, bias=self.eps.ap)
        self.rcp(rq.ap[:, 0:T], rq.ap[:, 0:T], [rq], [rq])
        for e in range(6):
            self.stt(cq[e].ap[:, 0:T], cq[e].ap[:, 0:T], self.qg.ap[:, e:e + 1], rq.ap[:, 0:T], ALU.mult, ALU.mult, [cq[e], self.qg, rq], [cq[e]])
        self.chk("B2")
        xabv = self.xab.ap[:, 0:ns * (L + 3)].rearrange("p (g l) -> p g l", g=ns)
        HS = {}

        def P1parts(h):
            self.p.tag = "P1"
            c = h
            H = HS[h] = {}
            sQ = self.rslot()
            wq = sQ.ap[:, 0:3 * 768].rearrange("p (a k e) -> p a k e", a=3, k=6)
            wqsrc = d["Wq"][l].rearrange("(k p) e -> p k e", p=128)
            self.ld(wq[:, 0], wqsrc[:, :, 128 * h:128 * h + 128], [self.DB(f"Wq{l}")], [sQ])
            self.ld(wq[:, 1], wqsrc[:, :, 1024 + 64 * h:1024 + 64 * h + 128], [self.DB(f"Wq{l}")], [sQ])
            self.ld(wq[:, 2], wqsrc[:, :, 1536 + 64 * h:1536 + 64 * h + 128], [self.DB(f"Wq{l}")], [sQ])
            sL = self.rslot()
            wl = sL.ap[:, 0:3072].rearrange("p (a k e) -> p a k e", a=3, k=8)
            wisrc = d["Win"][l].rearrange("(k p) e -> p k e", p=128)
            for a, nm in enumerate(("xa", "ga", "gb")):
                self.ld(wl[:, a], wisrc[:, :, OFF[nm] + 128 * c:OFF[nm] + 128 * c + 128], [self.DB(f"Win{l}")], [sL])
            if True:
                GA, GB = self.gb(), None
                for kc in range(6):
                    self.mm(GA.ap[:, 0:T], wq[:, 1, kc, :], cq[kc].ap[:, 0:T], kc == 0, kc == 5, [sQ, cq[kc]], [GA])
                t1, t2 = self.tf(), self.tf()
                self.tt(t1.ap[:, 0:T], GA.ap[:, 0:T], self.cosT.ap[:, 0:T], ALU.mult, [GA, self.cosT], [t1])
                yield
                self.p.tag = "P1"
                GB = self.gb()
                for kc in range(6):
                    self.mm(GB.ap[:, 0:T], wq[:, 2, kc, :], cq[kc].ap[:, 0:T], kc == 0, kc == 5, [sQ, cq[kc]], [GB])
                self.tt(t2.ap[:, 0:T], GB.ap[:, 0:T], self.ssinT.ap[:, 0:T], ALU.mult, [GB, self.ssinT], [t2])
                self.tt(self.qpe[h].ap[:, 0:T], t1.ap[:, 0:T], t2.ap[:, 0:T], ALU.add, [t1, t2], [self.qpe[h]])
            yield
            self.p.tag = "P1"
            Gq = self.gb()
            for kc in range(6):
                self.mm(Gq.ap[:, 0:T], wq[:, 0, kc, :], cq[kc].ap[:, 0:T], kc == 0, kc == 5, [sQ, cq[kc]], [Gq])
            qn = self.tb()
            self.act(qn.ap[:, 0:T], Gq.ap[:, 0:T], AF.Copy, [Gq], [qn])
            ql = self.qlat_p[h % 2] if prompt else self.qlat_s[h]
            yield
            self.p.tag = "P1"
            for rc in range(2):
                Gl = self.gb()
                self.mm(Gl.ap[:, 0:T], self.wukT.ap[:, h, rc * 128:(rc + 1) * 128], qn.ap[:, 0:T], True, True, [self.wukT, qn], [Gl])
                if rc == 0:
                    self.act(ql.ap[:, rc, 0:T], Gl.ap[:, 0:T], AF.Copy, [Gl], [ql])
                else:
                    self.cp(ql.ap[:, rc, 0:T], Gl.ap[:, 0:T], [Gl], [ql])
            H.update(ql=ql)
            yield
            self.p.tag = "P1"
            G = self.gb()
            for kc in range(8):
                self.mm(G.ap[:, 0:T], wl[:, 0, kc, :], hT.ap[:, kc, 0:T], kc == 0, kc == 7, [sL, hTc[kc]], [G])
            Gv = G.ap[:, 0:T].rearrange("p (g l) -> p g l", g=ns)
            self.cp(xabv[:, :, 0:3], self.carry.ap[:, c, 0:ns, 0:3], [self.carry], [self.xab])
            self.act(xabv[:, :, 3:3 + L], Gv, AF.Copy, [G], [self.xab])
            self.cp(self.carry.ap[:, c, 0:ns, 0:3], xabv[:, :, L:L + 3], [self.xab], [self.carry])
            if last_tile:
                self.cp(self.convlast.ap[:, c, 0:ns, :], Gv[:, :, L - 3:L], [G], [self.convlast])
            yield
            self.p.tag = "P1"
            G2 = self.gb()
            for kc in range(8):
                self.mm(G2.ap[:, 0:T], wl[:, 1, kc, :], hT.ap[:, kc, 0:T], kc == 0, kc == 7, [sL, hTc[kc]], [G2])
            tg, sg = self.tb(), self.tb()
            self.act(tg.ap[:, 0:T], G2.ap[:, 0:T], AF.Tanh, [G2], [tg], scale=0.5)
            self.stt(sg.ap[:, 0:T], tg.ap[:, 0:T], 1.0, G2.ap[:, 0:T], ALU.add, ALU.mult, [tg, G2], [sg])
            H.update(sg=sg, wl=wl, sL=sL)

        def P1(h):
            for _ in P1parts(h):
                pass

        def P1g(h):
            self.p.tag = "P1"
            c, H = h, HS[h]
            wl, sL = H["wl"], H["sL"]
            Gg = self.gb()
            for kc in range(8):
                self.mm(Gg.ap[:, 0:T], wl[:, 2, kc, :], hT.ap[:, kc, 0:T], kc == 0, kc == 7, [sL, hTc[kc]], [Gg])
            tgb = self.tb()
            self.act(tgb.ap[:, 0:T], Gg.ap[:, 0:T], AF.Tanh, [Gg], [tgb], scale=0.5)
            if prompt:
                sgb = self.sgbp[h % 2]
                self.stt(sgb.ap[:, 0:T], tgb.ap[:, 0:T], 1.0, Gg.ap[:, 0:T], ALU.add, ALU.mult, [tgb, Gg], [sgb])
            else:
                self.stt(self.sgb_s.ap[:, h, :], tgb.ap[:, 0:T], 1.0, Gg.ap[:, 0:T], ALU.add, ALU.mult, [tgb, Gg], [self.sgb_s])
            H.update(sgb=sgb if prompt else None)

        def P2(h):
            self.p.tag = "P2"
            c, H = h, HS[h]
            G3 = self.gb()
            for sgm in range(ns):
                for k in range(4):
                    self.mm(G3.ap[:, sgm * L:(sgm + 1) * L], self.diag.ap[:, c, k, :], xabv[:, sgm, k:k + L], k == 0, k == 3, [self.diag, self.xab], [G3])
            xcf, xcb = self.tf(), self.tb()
            self.act(xcf.ap[:, 0:T], G3.ap[:, 0:T], AF.Identity, [G3, v["conv_b"]], [xcf], bias=v["conv_b"].ap[:, c:c + 1])
            self.act(xcb.ap[:, 0:T], G3.ap[:, 0:T], AF.Identity, [G3, v["conv_b"]], [xcb], bias=v["conv_b"].ap[:, c:c + 1])
            H.update(xcf=xcf, xcb=xcb)

        def P3parts(h):
            self.p.tag = "P3"
            c, H = h, HS[h]
            xcf, xcb, sg = H["xcf"], H["xcb"], H["sg"]
            Gr = self.gb()
            self.mm(Gr.ap[:, 0:T], self.wa.ap[:, c, :], xcb.ap[:, 0:T], True, True, [self.wa, xcb], [Gr])
            Gi = self.gb()
            self.mm(Gi.ap[:, 0:T], self.wx.ap[:, c, :], xcb.ap[:, 0:T], True, True, [self.wx, xcb], [Gi])
            tr_, a_, a2, u = self.tf(), self.tf(), self.tf(), self.tf()
            ti = self.tb()
            self.act(tr_.ap[:, 0:T], Gr.ap[:, 0:T], AF.Tanh, [Gr, v["hba"]], [tr_], scale=0.5, bias=v["hba"].ap[:, c:c + 1])
            self.act(ti.ap[:, 0:T], Gi.ap[:, 0:T], AF.Tanh, [Gi, v["hbx"]], [ti], scale=0.5, bias=v["hbx"].ap[:, c:c + 1])
            yield
            self.p.tag = "P3"
            self.act(a_.ap[:, 0:T], tr_.ap[:, 0:T], AF.Exp, [tr_, v["hsp"]], [a_], scale=v["hsp"].ap[:, c:c + 1], bias=v["hsp"].ap[:, c:c + 1])
            self.act(a2.ap[:, 0:T], tr_.ap[:, 0:T], AF.Exp, [tr_, v["sp8"]], [a2], scale=v["sp8"].ap[:, c:c + 1], bias=v["sp8"].ap[:, c:c + 1])
            self.stt(u.ap[:, 0:T], ti.ap[:, 0:T], 1.0, xcf.ap[:, 0:T], ALU.add, ALU.mult, [ti, xcf], [u])
            yield
            self.p.tag = "P3"
            self.act(a2.ap[:, 0:T], a2.ap[:, 0:T], AF.Sqrt, [a2], [a2], scale=-1.0, bias=self.one.ap)
            self.stt(u.ap[:, 0:T], u.ap[:, 0:T], 0.5, a2.ap[:, 0:T], ALU.mult, ALU.mult, [u, a2], [u])
            hs = self.tf()
            for sgm in range(ns):
                cs_ = slice(sgm * L, (sgm + 1) * L)
                hc = self.hcar.ap[:, c, sgm:sgm + 1]
                self.p.op("dve", (lambda o, a0, b0, i0: (lambda e: e.tensor_tensor_scan(out=o, data0=a0, data1=b0, initial=i0, op0=ALU.mult, op1=ALU.add)))(
                    hs.ap[:, cs_], a_.ap[:, cs_], u.ap[:, cs_], hc), [a_, u, self.hcar], [hs])
                self.cp(hc, hs.ap[:, (sgm + 1) * L - 1:(sgm + 1) * L], [hs], [self.hcar])
            self.stt(self.ya[c].ap[:, 0:T], hs.ap[:, 0:T], 0.5, sg.ap[:, 0:T], ALU.mult, ALU.mult, [hs, sg], [self.ya[c]])

        def P3(h):
            for _ in P3parts(h):
                pass

        def UV(h):
            self.p.tag = "UV"
            sgb = HS[h]["sgb"]
            Gu = self.gb()
            for rc in range(2):
                self.mm(Gu.ap[:, 0:T], self.wuv.ap[:, rc, h * 128:(h + 1) * 128], self.olat.ap[:, rc, 0:T], rc == 0, rc == 1, [self.wuv, self.olat], [Gu])
            self.stt(self.yb[h].ap[:, 0:T], Gu.ap[:, 0:T], 0.5, sgb.ap[:, 0:T], ALU.mult, ALU.mult, [Gu, sgb], [self.yb[h]])

        if prompt:
            nk = 4 * t + 4
            fin = {}
            for h in range(9):
                if h == 0:
                    P1(0); P1g(0); P2(0); P3(0)
                    continue
                hooks = {}

                def add(kt, fn):
                    hooks.setdefault(kt, []).append(fn)
                uv_kt = min(5, nk - 1)
                if h < 8:
                    uv_kt = (8 * nk) // 13
                if h >= 2:
                    fg = fin[h - 2]()
                    fstep = lambda g_=fg: next(g_, None)
                    for i in range(4):
                        add(min(1 + i, max(uv_kt - 1, 1)), fstep)
                if h < 8:
                    gen = P1parts(h)
                    step = lambda g_=gen: next(g_, None)
                    g3 = P3parts(h)
                    step3 = lambda g_=g3: next(g_, None)
                    parts = [step] * 6 + [None, (lambda hh=h: P2(hh)), None, step3, step3, step3]
                    for i, fn in enumerate(parts):
                        kt = (i * nk) // 13 if i < 8 else ((i + 1) * nk) // 13
                        if i == 6:
                            if h >= 2:
                                add(uv_kt, lambda hh=h: UV(hh - 2))
                            add(uv_kt, lambda hh=h: P1g(hh))
                        elif fn is not None:
                            add(kt, fn)
                elif h >= 2:
                    add(uv_kt, lambda hh=h: UV(hh - 2))
                fin[h - 1] = self.attn_prompt(t, h - 1, HS[h - 1]["ql"], hooks)
            for _ in fin[7]():
                pass
            UV(7)
        else:
            for h in range(8):
                P1(h); P1g(h); P2(h); P3(h)
        self.chk("heads")
        if not prompt:
            self.attn_sample()
        self.p.tag = "MERGE"
        mg = self.mg
        for c in range(8):
            sM = self.rslot()
            wm = sM.ap.rearrange("p (a k e) -> p a k e", a=4, k=8)
            for a, (nm, c0) in enumerate((("Wa", 128 * c), ("Wb", 128 * c), ("Win", OFF["ua"] + 128 * c), ("Win", OFF["ub"] + 128 * c))):
                self.ld(wm[:, a], d[nm][l].rearrange("(k p) e -> p k e", p=128)[:, :, c0:c0 + 128], [self.DB(f"{nm}{l}")], [sM])
            Gua = self.gb()
            for kc in range(8):
                self.mm(Gua.ap[:, 0:T], wm[:, 2, kc, :], hT.ap[:, kc, 0:T], kc == 0, kc == 7, [sM, hTc[kc]], [Gua])
            tua, tub = self.tb(), self.tb()
            self.act(tua.ap[:, 0:T], Gua.ap[:, 0:T], AF.Tanh, [Gua], [tua], scale=0.5)
            Gub = self.gb()
            for kc in range(8):
                self.mm(Gub.ap[:, 0:T], wm[:, 3, kc, :], hT.ap[:, kc, 0:T], kc == 0, kc == 7, [sM, hTc[kc]], [Gub])
            self.act(tub.ap[:, 0:T], Gub.ap[:, 0:T], AF.Tanh, [Gub], [tub], scale=0.5)
            m1, m2 = self.tf(), self.tf()
            GA = self.gb()
            for kc in range(8):
                self.mm(GA.ap[:, 0:T], wm[:, 0, kc, :], self.ya[kc].ap[:, 0:T], kc == 0, kc == 7, [sM, self.ya[kc]], [GA])
            self.stt(m1.ap[:, 0:T], tua.ap[:, 0:T], 1.0, GA.ap[:, 0:T], ALU.add, ALU.mult, [tua, GA], [m1])
            GB = self.gb()
            for kc in range(8):
                self.mm(GB.ap[:, 0:T], wm[:, 1, kc, :], self.yb[kc].ap[:, 0:T], kc == 0, kc == 7, [sM, self.yb[kc]], [GB])
            self.stt(m2.ap[:, 0:T], tub.ap[:, 0:T], 1.0, GB.ap[:, 0:T], ALU.add, ALU.mult, [tub, GB], [m2])
            self.tt(m1.ap[:, 0:T], m1.ap[:, 0:T], m2.ap[:, 0:T], ALU.add, [m1, m2], [m1])
            self.act(mg[c].ap[:, 0:T], m1.ap[:, 0:T], AF.Copy, [m1], [mg[c]], scale=0.5)
        self.chk("merge")
        self.p.tag = "OUT"
        so0, wo0 = self.wload("Wo", l, 1024, 0, 512)
        so1, wo1 = self.wload("Wo", l, 1024, 512, 512)
        genA = None
        if prompt and t < g.nt - 1:
            genA = self.stageA_gen(t + 1)
        for s in range(nsub):
            if genA is not None:
                next(genA, None)
                self.p.tag = "OUT"
            xt = self.xt[s % 2]
            r0 = row0 + s * 128
            self.ld(xt.ap[0:tsz, :], xsrc[r0:r0 + tsz, :], [self.DB(xn_(s))] if l == 1 else [], [xt])
            Gs = []
            for half, (so, wo) in enumerate(((so0, wo0), (so1, wo1))):
                G = self.gb()
                for kc in range(8):
                    self.mm(G.ap[0:tsz, :], mg[kc].ap[:, s * 128:s * 128 + tsz], wo[:, kc, :], kc == 0, kc == 7, [mg[kc], so], [G])
                Gs.append(G)
            st, junk = self.st(), self.tb()
            self.act(junk.ap[0:tsz, :], Gs[0].ap[0:tsz, :], AF.Square, [Gs[0]], [junk, st], accum=st.ap[0:tsz, 0:1])
            self.act(junk.ap[0:tsz, :], Gs[1].ap[0:tsz, :], AF.Square, [Gs[1]], [junk, st], accum=st.ap[0:tsz, 1:2])
            self.tt(st.ap[0:tsz, 0:1], st.ap[0:tsz, 0:1], st.ap[0:tsz, 1:2], ALU.add, [st], [st])
            self.act(st.ap[0:tsz, 1:2], st.ap[0:tsz, 0:1], AF.Sqrt, [st], [st], scale=1.0 / D, bias=self.eps.ap[0:tsz, :])
            self.rcp(st.ap[0:tsz, 2:3], st.ap[0:tsz, 1:2], [st], [st])
            for half in range(2):
                tmp = self.tf()
                hs_ = slice(half * 512, (half + 1) * 512)
                self.stt(tmp.ap[0:tsz, :], Gs[half].ap[0:tsz, :], st.ap[0:tsz, 2:3], self.GG.ap[0:tsz, hs_], ALU.mult, ALU.mult, [Gs[half], st, self.GG], [tmp])
                self.tt(xt.ap[0:tsz, hs_], xt.ap[0:tsz, hs_], tmp.ap[0:tsz, :], ALU.add, [xt, tmp], [xt])
            self.ld(ydst[r0:r0 + tsz, :], xt.ap[0:tsz, :], [xt], [self.DB(xn_(s)) if l == 0 else self.DB("o_y")], q="act", final=(l == 1))
        if genA is not None:
            for _ in genA:
                pass
            self.stageA_done = (id(g), t + 1)

    def attn_prompt(self, t, h, ql, hooks=None):
        hooks = hooks or {}
        nk = 4 * t + 4
        qpe = self.qpe[h]
        O, mk = self.O, self.maskt
        acc = self.accp[h % 2]

        def PV(kt, c0, P, KM):
            for rc in range(2):
                self.mm(O[rc].ap[:, c0:512], KM.ap[:, rc * 128:(rc + 1) * 128], P.ap[:, c0:512], kt == 0, kt == nk - 1, [KM, P], [O[rc]])

        self.p.tag = "ATT"
        pend = []
        Sb = [self.S[0], self.S[1], self.O[2]]
        for kt in range(nk):
            j = kt - 4 * t
            c0 = 128 * j if j > 0 else 0
            S = Sb[kt % 3]
            KT, KP, KM = self.K_T[kt], self.K_pe[kt], self.K_tm[kt]
            self.mm(S.ap[:, c0:512], KT.ap[:, 0, :], ql.ap[:, 0, c0:512], True, False, [KT, ql], [S])
            self.mm(S.ap[:, c0:512], KT.ap[:, 1, :], ql.ap[:, 1, c0:512], False, False, [KT, ql], [S])
            self.mm(S.ap[:, c0:512], KP.ap, qpe.ap[:, c0:512], False, j < 0, [KP, qpe], [S])
            if j >= 0:
                self.mm(S.ap[:, c0:c0 + 128], mk.ap[:, 0:128], mk.ap[:, 128:256], False, True, [mk], [S])
            P = self.tp()
            self.act(P.ap[:, c0:512], S.ap[:, c0:512], AF.Exp, [S], [P], scale=SCALE)
            fuse01 = nk >= 2 and (1 - 4 * t) <= 0
            if kt == 0:
                P0 = P
                if not fuse01:
                    self.cp(acc.ap, P.ap, [P], [acc], eng="pool")
            elif kt == 1 and fuse01:
                self.tt(acc.ap, P0.ap, P.ap, ALU.add, [P0, P], [acc], eng="pool")
            else:
                self.tt(acc.ap[:, c0:512], acc.ap[:, c0:512], P.ap[:, c0:512], ALU.add, [acc, P], [acc], eng="pool")
            pend.append((kt, c0, P, KM))
            if len(pend) > 2:
                PV(*pend.pop(0))
            for fn in hooks.get(kt, ()):
                fn()
                self.p.tag = "ATT"
        for pv in pend:
            PV(*pv)
        for rc in range(2):
            self.act(self.olat.ap[:, rc, :], O[rc].ap, AF.Copy, [O[rc]], [self.olat])

        def finalize():
            self.p.tag = "FIN"
            Gs = self.gb()
            self.mm(Gs.ap, self.ones_ff.ap, acc.ap, True, True, [self.ones_ff, acc], [Gs])
            rs = self.tf()
            self.cp(rs.ap, Gs.ap, [Gs], [rs])
            for q4 in range(4):
                self.rcp(rs.ap[:, q4 * 128:(q4 + 1) * 128], rs.ap[:, q4 * 128:(q4 + 1) * 128], [rs], [rs])
                if q4 < 3:
                    yield
                    self.p.tag = "FIN"
            for rc in range(2):
                self.tt(self.olat.ap[:, rc, :], self.olat.ap[:, rc, :], rs.ap, ALU.mult, [self.olat, rs], [self.olat])
        return finalize

    def attn_sample(self):
        self.p.tag = "ATTS"
        g, d = self.g, self.d
        l = g.l
        O, mk = self.O, self.maskt
        idb = self.ident_b.ap
        for sgm in range(4):
            self.ld(self.ckvtm_all.ap[:, 0:16, :], d["cckv"][l, sgm].rearrange("(j p) r -> p j r", p=128), [], self.K_tm[0:16], q="pool", chan="kc_ckv")
            slot = self.rslot()
            kst = slot.ap[:, 0:2048].rearrange("p (j e) -> p j e", j=16)
            self.ld(kst[:, :, 0:64], d["ckpe"][l, sgm].rearrange("(j p) e -> p j e", p=128), [], [slot], q="pool")
            self.p.op("dve", (lambda a: (lambda e: e.memset(a, 0.0)))(kst[:, :, 64:128]), [], [slot])
            for kt in range(16):
                KM, KT, KP = self.K_tm[kt], self.K_T[kt], self.K_pe[kt]
                TBk = self.tbk()
                self.tr(TBk.ap[:, 0:128], KM.ap[:, 0:128], idb, [KM, self.ident_b], [TBk])
                self.tr(TBk.ap[:, 128:256], KM.ap[:, 128:256], idb, [KM, self.ident_b], [TBk])
                self.tr(TBk.ap[:, 256:384], kst[:, kt, :], idb, [slot, self.ident_b], [TBk])
                self.act(KT.ap[:, 0, :], TBk.ap[:, 0:128], AF.Copy, [TBk], [KT])
                self.act(KT.ap[:, 1, :], TBk.ap[:, 128:256], AF.Copy, [TBk], [KT])
                self.act(KP.ap, TBk.ap[:, 256:384], AF.Copy, [TBk], [KP])
            cols = slice(sgm * 16, sgm * 16 + 16)

            def PV(kt, nk, P, KM):
                for rc in range(2):
                    self.mm(O[rc].ap[:, 0:128], KM.ap[0:nk, rc * 128:(rc + 1) * 128], P.ap[0:nk, 0:128], kt == 0, kt == 16, [KM, P], [O[rc]])
                self.mm(O[2].ap[:, 0:128], self.ones_b.ap[0:nk, :], P.ap[0:nk, 0:128], kt == 0, kt == 16, [self.ones_b, P], [O[2]])

            prev = None
            for kt in range(17):
                nk = 128 if kt < 16 else 64
                S = self.S[kt % 2]
                KM, KT, KP = self.K_tm[kt], self.K_T[kt], self.K_pe[kt]
                for h in range(8):
                    qpe, ql = self.qpe[h], self.qlat_s[h]
                    so = S.ap[0:nk, h * 16:(h + 1) * 16]
                    self.mm(so, KT.ap[:, 0, 0:nk], ql.ap[:, 0, cols], True, False, [KT, ql], [S])
                    self.mm(so, KT.ap[:, 1, 0:nk], ql.ap[:, 1, cols], False, False, [KT, ql], [S])
                    self.mm(so, KP.ap[:, 0:nk], qpe.ap[:, cols], False, kt < 16, [KP, qpe], [S])
                    if kt == 16:
                        self.mm(so, mk.ap[:, 384 + sgm * 64:384 + (sgm + 1) * 64], mk.ap[:, 256:272], False, True, [mk], [S])
                P = self.tp()
                self.act(P.ap[0:nk, 0:128], S.ap[0:nk, 0:128], AF.Exp, [S], [P], scale=SCALE)
                if prev is not None:
                    PV(*prev)
                prev = (kt, nk, P, KM)
            PV(*prev)
            rs = self.tf()
            self.rcp(rs.ap[:, 0:128], O[2].ap[:, 0:128], [O[2]], [rs])
            for rc in range(2):
                self.tt(self.olat.ap[:, rc, sgm * 128:(sgm + 1) * 128], O[rc].ap[:, 0:128], rs.ap[:, 0:128], ALU.mult, [O[rc], rs], [self.olat])
        for h in range(8):
            Gu = self.gb()
            for sgm in range(4):
                for rc in range(2):
                    self.mm(Gu.ap[:, sgm * 16:(sgm + 1) * 16], self.wuv.ap[:, rc, h * 128:(h + 1) * 128],
                            self.olat.ap[:, rc, sgm * 128 + h * 16:sgm * 128 + (h + 1) * 16], rc == 0, rc == 1, [self.wuv, self.olat], [Gu])
            self.stt(self.yb[h].ap[:, 0:64], Gu.ap[:, 0:64], 0.5, self.sgb_s.ap[:, h, :], ALU.mult, ALU.mult, [Gu, self.sgb_s], [self.yb[h]])


_CACHE = {}


def _tables():
    half = 32
    inv = (np.float32(10000.0) ** (-np.arange(half, dtype=np.float32) / np.float32(half))).astype(np.float32)
    pos = np.arange(4096, dtype=np.float32)
    ang = (pos[:, None] * inv[None, :]).astype(np.float32)
    cos, sin = np.cos(ang).astype(np.float32), np.sin(ang).astype(np.float32)
    cos_tm = np.concatenate([cos, cos], 1)
    ssin_tm = np.concatenate([-sin, sin], 1)
    cosT = np.ascontiguousarray(np.concatenate([cos_tm, cos_tm], 1).T)
    ssinT = np.ascontiguousarray(np.concatenate([ssin_tm, ssin_tm], 1).T)
    ps = np.tile(np.arange(2048, 2064), 4)
    mrow = np.zeros((1, 1024), np.float32)
    mrow[0, 64:128] = 1.0
    mrow[0, 128:192] = NEG
    mrow[0, 256:384] = NEG
    for g_ in range(4):
        u = np.ones(64, np.float32)
        u[g_ * 16:(g_ + 1) * 16] = 0.0
        mrow[0, 384 + g_ * 64:384 + (g_ + 1) * 64] = u
    sel = np.zeros((4, 64), np.float32)
    for g_ in range(4):
        sel[g_, g_ * 16:(g_ + 1) * 16] = 1.0
    return dict(ident=np.eye(128, dtype=np.float32), cosT=cosT, ssinT=ssinT, cos_tm=cos_tm, ssin_tm=ssin_tm,
                cosT_s=np.ascontiguousarray(cosT[:, ps]), ssinT_s=np.ascontiguousarray(ssinT[:, ps]),
                cos_tm_s=np.ascontiguousarray(cos_tm[ps]), ssin_tm_s=np.ascontiguousarray(ssin_tm[ps]),
                sel_s=sel, mrow=mrow)


def kernel(**inp):
    f = lambda a: np.ascontiguousarray(np.asarray(a, dtype=np.float32))
    if "mk" not in _CACHE:
        _CACHE["mk"] = MK()
    mk = _CACHE["mk"]
    tabs = _tables()
    wnames = ["ada_w", "ada_b", "pre_norm", "post_norm", "w_in", "conv_w", "conv_b", "lru_wa", "lru_ba", "lru_wx",
              "lru_bx", "lru_lambda", "q_norm", "w_q_up", "kv_norm", "w_branch_a", "w_branch_b", "w_out"]
    shared = {n: f(inp[n]) for n in wnames}
    shared["w_uk"] = f(inp["w_uk"]).reshape(2, 256, 1024)
    shared["w_uv"] = f(inp["w_uv"]).reshape(2, 256, 1024)
    shared.update(tabs)
    in_maps = []
    for c in range(8):
        m = dict(shared)
        m["xp"] = f(inp["x_prompt"][2 * c:2 * c + 2])
        m["xs"] = f(inp["x_sample"][4 * c:4 * c + 4]).reshape(64, D)
        m["cp"] = f(inp["c_prompt"][2 * c:2 * c + 2])
        m["cs"] = f(inp["c_sample"][4 * c:4 * c + 4])
        m["cckv"] = f(inp["cache_ckv"][:, 4 * c:4 * c + 4])
        m["ckpe"] = f(inp["cache_kpe"][:, 4 * c:4 * c + 4])
        m["sconv"] = f(inp["state_conv"][:, 4 * c:4 * c + 4])
        m["slru"] = f(inp["state_lru"][:, 4 * c:4 * c + 4])
        in_maps.append(m)
    res = run_bass_kernel_spmd(mk.nc, in_maps, core_ids=list(range(8)))
    R = res.results
    cat = lambda k, ax: np.concatenate([np.asarray(r[k]) for r in R], axis=ax)
    yp = cat("yp", 0)
    ys = cat("ys", 0).reshape(32, 16, D)
    nckv_p = cat("nckv_p", 1)
    nkpe_p = cat("nkpe_p", 1)
    nconv_p = cat("nconv_p", 1)
    nlru_p = cat("nlru_p", 1)
    nckv_s = np.concatenate([np.asarray(r["nckv_s"]).reshape(2, 4, 16, 256) for r in R], 1)
    nkpe_s = np.concatenate([np.asarray(r["nkpe_s"]).reshape(2, 4, 16, 64) for r in R], 1)
    nconv_s = cat("nconv_s", 1)
    nlru_s = cat("nlru_s", 1)
    return tuple(np.ascontiguousarray(a, dtype=np.float32) for a in
                 (yp, ys, nckv_p, nkpe_p, nconv_p, nlru_p, nckv_s, nkpe_s, nconv_s, nlru_s))
```

```python
import numpy as np
import concourse.bass as bass
import concourse.mybir as mybir
from concourse.bass_utils import run_bass_kernel_spmd

F32 = mybir.dt.float32
BF16 = mybir.dt.bfloat16
AF = mybir.ActivationFunctionType
ALU = mybir.AluOpType

EPOCH = 16000


class Buf:
    __slots__ = ("name", "ap", "w", "r")

    def __init__(self, name, ap=None):
        self.name = name
        self.ap = ap
        self.w = {}
        self.r = {}

    def view(self, ap):
        v = Buf(self.name, ap)
        v.w, v.r = self.w, self.r
        return v


class Op:
    __slots__ = ("eng", "fn", "deps", "mark", "cnt", "chan", "val", "idx", "tag")

    def __init__(self, eng, fn, chan=None):
        self.eng = eng
        self.fn = fn
        self.deps = []
        self.mark = False
        self.cnt = None
        self.chan = chan
        self.val = None
        self.idx = None
        self.tag = ""


class Prog:
    ENGS = ("pe", "act", "dve", "pool", "sp")

    def __init__(self, nc):
        self.nc = nc
        self.ops = {e: [] for e in self.ENGS}
        self.chan_cnt = {}
        self.finals = []
        self.nbuf = 0

    def sb(self, name, shape, dt):
        t = self.nc.alloc_sbuf_tensor("sb_" + name, shape, dt)
        return Buf(name, t.ap())

    def ps(self, name, shape, dt=F32):
        t = self.nc.alloc_psum_tensor("ps_" + name, shape, dt)
        return Buf(name, t.ap())

    def _track(self, op, reads, writes):
        key = op.chan if op.chan is not None else op.eng
        isdma = op.chan is not None
        same = key in ("act", "dve", "pool")
        deps = op.deps
        for b in reads:
            for k, o in b.w.items():
                if k != key or same or isdma:
                    deps.append(o)
        for b in writes:
            for k, o in b.w.items():
                if (k != key or same) and o is not op:
                    deps.append(o)
            for k, o in b.r.items():
                if (k != key or same or isdma) and o is not op:
                    deps.append(o)
        for b in reads:
            b.r[key] = op
        for b in writes:
            if isdma:
                b.w.clear()
            else:
                for k in [k for k in b.w if not isinstance(k, tuple)]:
                    del b.w[k]
            b.w[key] = op
            b.r.clear()

    tag = ""

    def op(self, eng, fn, reads=(), writes=()):
        o = Op(eng, fn)
        o.tag = self.tag
        o.idx = len(self.ops[eng])
        self._track(o, reads, writes)
        self.ops[eng].append(o)
        return o

    def dma(self, q, out, in_, reads=(), writes=(), chan=None, final=False, **kw):
        if q == "pool":
            self.nsw = getattr(self, "nsw", 0) + 1
            chan = f"sw{self.nsw}"
        elif chan is None:
            cands = [b for b in list(writes) + list(reads) if b.ap is not None]
            chan = "d_" + cands[0].name
        chan = ("dma", chan)
        o = Op(q, lambda e: e.dma_start(out=out, in_=in_, **kw), chan=chan)
        o.idx = len(self.ops[q])
        self.chan_cnt[chan] = self.chan_cnt.get(chan, 0) + 16
        o.val = self.chan_cnt[chan]
        self._track(o, reads, writes)
        self.ops[q].append(o)
        if final:
            self.finals.append(o)
        return o

    def emit(self):
        nc = self.nc
        for e in self.ENGS:
            for o in self.ops[e]:
                for d in o.deps:
                    if d.chan is None:
                        d.mark = True
        esems = {}
        for e in self.ENGS:
            c = 0
            for o in self.ops[e]:
                if o.chan is None and o.mark:
                    c += 1
                    o.cnt = c
            nep = c // EPOCH + 1
            esems[e] = [nc.alloc_semaphore(f"s_{e}_{i}") for i in range(nep)]
        csems = {ch: nc.alloc_semaphore("c_" + ch[1]) for ch in self.chan_cnt}

        def token(d):
            if d.chan is not None:
                return (csems[d.chan], d.val, d.chan)
            ep, v = divmod(d.cnt - 1, EPOCH)
            return (esems[d.eng][ep], v + 1, (d.eng, ep))

        fin = {}
        for o in self.finals:
            s_, v_, k_ = token(o)
            if k_ not in fin or fin[k_][1] < v_:
                fin[k_] = (s_, v_, k_)
        finals = list(fin.values())
        nwaits = [0]

        def run(e, eng):
            seen = {}
            lst = self.ops[e]
            for o in lst:
                need = {}
                for d in o.deps:
                    s, v, k = token(d)
                    if seen.get(k, 0) >= v:
                        continue
                    if k not in need or need[k][1] < v:
                        need[k] = (s, v)
                for k, (s, v) in need.items():
                    eng.wait_ge(s, v)
                    seen[k] = v
                    nwaits[0] += 1
                ins = o.fn(eng)
                if o.chan is not None:
                    ins.then_inc(csems[o.chan], 16)
                elif o.mark:
                    ep = (o.cnt - 1) // EPOCH
                    ins.then_inc(esems[e][ep], 1)
            if e == "sp":
                for s, v, k in finals:
                    eng.wait_ge(s, v)

        with nc.Block() as block:
            @block.tensor
            def _(eng):
                run("pe", eng)

            @block.scalar
            def _(eng):
                run("act", eng)

            @block.vector
            def _(eng):
                run("dve", eng)

            @block.gpsimd
            def _(eng):
                run("pool", eng)

            @block.sync
            def _(eng):
                run("sp", eng)
        self.stats = {e: len(self.ops[e]) for e in self.ENGS}
        self.stats["waits"] = nwaits[0]
        self.stats["sems"] = sum(len(v) for v in esems.values()) + len(csems)


D = 1024
CQ = 768
OFF = dict(xa=0, ga=1024, cq=2048, ckv=2816, kpe=3072, gb=3136, ua=4160, ub=5184)
IN_DIM = 6208
SCALE = float(192 ** -0.5)
EPS = 1e-6
NEG = -30000.0
PAST = 2048
FAST_RCP = False


class MK:
    def __init__(self, SEQ=4096, NPS=2, do_sample=True, stop=None):
        self.SEQ, self.NPS, self.do_sample = SEQ, NPS, do_sample
        self.stop = stop
        nc = self.nc = bass.Bass("TRN2", target_bir_lowering=False)
        p = self.p = Prog(nc)
        I = lambda n, s, dt=F32: nc.dram_tensor(n, list(s), dt, kind="ExternalInput").ap()
        Oo = lambda n, s, dt=F32: nc.dram_tensor(n, list(s), dt, kind="ExternalOutput").ap()
        N = lambda n, s, dt=F32: nc.dram_tensor(n, list(s), dt, kind="Internal").ap()
        d = self.d = {}
        for n, s in [("xp", (NPS, SEQ, D)), ("xs", (64, D)), ("cp", (NPS, D)), ("cs", (4, D)),
                     ("cckv", (2, 4, PAST, 256)), ("ckpe", (2, 4, PAST, 64)), ("sconv", (2, 4, 3, D)),
                     ("slru", (2, 4, D)), ("ada_w", (2, D, 3 * D)), ("ada_b", (2, 3 * D)),
                     ("pre_norm", (2, D)), ("post_norm", (2, D)), ("w_in", (2, D, IN_DIM)),
                     ("conv_w", (2, 4, D)), ("conv_b", (2, D)), ("lru_wa", (2, 8, 128, 128)),
                     ("lru_ba", (2, D)), ("lru_wx", (2, 8, 128, 128)), ("lru_bx", (2, D)),
                     ("lru_lambda", (2, D)), ("q_norm", (2, CQ)), ("w_q_up", (2, CQ, 1536)),
                     ("kv_norm", (2, 256)), ("w_uk", (2, 256, 1024)), ("w_uv", (2, 256, 1024)),
                     ("w_branch_a", (2, D, D)), ("w_branch_b", (2, D, D)), ("w_out", (2, D, D)),
                     ("ident", (128, 128)), ("cosT", (128, 4096)), ("ssinT", (128, 4096)),
                     ("cos_tm", (4096, 64)), ("ssin_tm", (4096, 64)),
                     ("cosT_s", (128, 64)), ("ssinT_s", (128, 64)), ("cos_tm_s", (64, 64)),
                     ("ssin_tm_s", (64, 64)), ("sel_s", (4, 64)), ("mrow", (1, 1024))]:
            d[n] = I(n, s)
        for n, s in [("yp", (NPS, SEQ, D)), ("ys", (64, D)), ("nckv_p", (2, NPS, SEQ, 256)),
                     ("nkpe_p", (2, NPS, SEQ, 64)), ("nconv_p", (2, NPS, 3, D)), ("nlru_p", (2, NPS, D)),
                     ("nckv_s", (2, 64, 256)), ("nkpe_s", (2, 64, 64)), ("nconv_s", (2, 4, 3, D)),
                     ("nlru_s", (2, 4, D))]:
            d[n] = Oo(n, s)
        d["x1p"] = N("x1p", (NPS, SEQ, D))
        d["x1s"] = N("x1s", (64, D))
        d["Win"] = N("Win", (2, D, IN_DIM), BF16)
        d["Wq"] = N("Wq", (2, CQ, 2112), BF16)
        d["Wa"] = N("Wa", (2, D, D), BF16)
        d["Wb"] = N("Wb", (2, D, D), BF16)
        d["Wo"] = N("Wo", (2, D, D), BF16)
        self.dbuf = {}
        self.alloc()
        try:
            self.setup()
            self.chk("setup")
            for s in range(NPS):
                for l in range(2):
                    self.run_pass(l, "p", s)
            if do_sample:
                for l in range(2):
                    self.run_pass(l, "s", 0)
        except StopIteration:
            pass
        p.emit()

    def chk(self, name):
        if self.stop == name:
            raise StopIteration

    def DB(self, name):
        if name not in self.dbuf:
            self.dbuf[name] = Buf(name)
        return self.dbuf[name]

    def alloc(self):
        p = self.p
        sb = p.sb
        self.ident_f = sb("ident_f", [128, 128], F32)
        self.ident_b = sb("ident_b", [128, 128], BF16)
        self.ones_b = sb("ones_b", [128, 128], BF16)
        self.ones_f = sb("ones_f", [1, 128], F32)
        self.ones_ff = sb("ones_ff", [128, 128], F32)
        self.accp = [sb(f"accp{i}", [128, 512], F32) for i in range(2)]
        self.mrow_f = sb("mrow_f", [1, 1024], F32)
        self.mrow = sb("mrow", [1, 1024], BF16)
        self.sel_s = sb("sel_s", [4, 64], F32)
        ckvT = sb("ckvT", [128, 2, 4096], BF16)
        ckvtm = sb("ckvtm", [128, 32, 256], BF16)
        kpeT = sb("kpeT", [128, 4096], BF16)
        self.K_T = [Buf(f"ckvT{k}", ckvT.ap[:, :, k * 128:(k + 1) * 128]) for k in range(32)]
        self.K_tm = [Buf(f"ckvtm{k}", ckvtm.ap[:, k, :]) for k in range(32)]
        self.K_pe = [Buf(f"kpeT{k}", kpeT.ap[:, k * 128:(k + 1) * 128]) for k in range(32)]
        self.ckvT_all, self.ckvtm_all, self.kpeT_all = ckvT, ckvtm, kpeT
        self.xt = [sb(f"xt{i}", [128, 1024], F32) for i in range(2)]
        self.xn = [sb(f"xn{i}", [128, 1024], BF16) for i in range(2)]
        self.hT = sb("hT", [128, 8, 512], BF16)
        self.hTc = [Buf(f"hT{c}", self.hT.ap[:, c, :]) for c in range(8)]
        self.ring = [sb(f"ring{i}", [128, 4096], BF16) for i in range(3)]
        self.ring_i = 0
        cq = sb("cq", [128, 6, 512], BF16)
        self.cq = [Buf(f"cq{e}", cq.ap[:, e, :]) for e in range(6)]
        qlat = sb("qlat", [128, 2, 2, 512], BF16)
        self.qlat_p = [Buf(f"qlatp{i}", qlat.ap[:, i]) for i in range(2)]
        qls = sb("qlat_s", [128, 8, 2, 64], BF16)
        self.qlat_s = [Buf(f"qlats{i}", qls.ap[:, i]) for i in range(8)]
        qpe = sb("qpe", [128, 8, 512], BF16)
        self.qpe = [Buf(f"qpe{j}", qpe.ap[:, j, :]) for j in range(8)]
        self.maskt = sb("maskt", [128, 640], BF16)
        self.cosT = sb("cosT_t", [128, 512], F32)
        self.ssinT = sb("ssinT_t", [128, 512], F32)
        self.cos_tm = sb("cos_tm_t", [128, 4, 64], F32)
        self.ssin_tm = sb("ssin_tm_t", [128, 4, 64], F32)
        ya = sb("yaT", [128, 8, 512], BF16)
        yb = sb("ybT", [128, 8, 512], BF16)
        mg = sb("mgT", [128, 8, 512], BF16)
        self.ya = [Buf(f"ya{c}", ya.ap[:, c, :]) for c in range(8)]
        self.yb = [Buf(f"yb{c}", yb.ap[:, c, :]) for c in range(8)]
        self.mg = [Buf(f"mg{c}", mg.ap[:, c, :]) for c in range(8)]
        self.GG = sb("GG", [128, 1024], F32)
        self.ckv_st = [sb(f"ckv_st{i}", [128, 256], F32) for i in range(2)]
        self.kpe_st = [sb(f"kpe_st{i}", [128, 64], F32) for i in range(2)]
        self.kpedup = sb("kpedup", [128, 128], BF16)
        self.wukT = sb("wukT", [128, 8, 256], BF16)
        self.wuv = sb("wuv", [128, 2, 1024], BF16)
        self.wa = sb("wa", [128, 8, 128], BF16)
        self.wx = sb("wx", [128, 8, 128], BF16)
        self.diag = sb("diag", [128, 8, 4, 128], BF16)
        self.cw = sb("cw", [128, 8, 4], F32)
        self.vec = {n: sb("v_" + n, [128, 8], F32) for n in
                    ("conv_b", "lru_ba", "lru_bx", "lru_lambda", "pre_norm", "post_norm", "hba", "hbx",
                     "sp", "sp8", "hsp")}
        self.qg = sb("qg", [128, 6], F32)
        self.kvg = sb("kvg", [128, 256], F32)
        self.adab = sb("adab", [128, 24], F32)
        self.modT = sb("modT", [128, 24, 4], F32)
        self.gs = sb("gs", [128, 8, 4], F32)
        self.ggT = sb("ggT", [128, 8, 4], F32)
        self.gg_tm = self.c_sb = self.xt[0]
        self.c_t = self.xt[1]
        self.eps = sb("eps", [128, 1], F32)
        self.one = sb("one", [128, 1], F32)
        self.scT = sb("scT", [128, 8, 4], F32)
        self.hcar = sb("hcar", [128, 8, 4], F32)
        self.carry = sb("carry", [128, 8, 4, 4], BF16)
        self.carry_f = sb("carry_f", [128, 8, 4, 3], F32)
        self.convlast = sb("convlast", [128, 8, 4, 3], F32)
        self.stat = [sb(f"stat{i}", [128, 4], F32) for i in range(4)]
        self.stat_i = 0
        self.xab = sb("xab", [128, 516], BF16)
        self.sgb_s = sb("sgb_s", [128, 8, 64], BF16)
        self.olat = sb("olat", [128, 2, 512], BF16)
        NF, NB = 7, 6
        self.ppool = [sb(f"pp{i}", [128, 512], BF16) for i in range(3)]
        self.pi = 0
        self.sgbp = [sb(f"sgbp{i}", [128, 512], BF16) for i in range(2)]
        self.fpool = [sb(f"fp{i}", [128, 512], F32) for i in range(NF)]
        self.bpool = [sb(f"bp{i}", [128, 512], BF16) for i in range(NB)]
        self.fi = self.bi = 0
        ps = p.ps
        self.S = [ps(f"S{i}", [128, 512]) for i in range(2)]
        self.O = [ps(f"O{i}", [128, 512]) for i in range(3)]
        self.G = [ps(f"G{i}", [128, 512]) for i in range(3)]
        self.TB = [b.view(b.ap.bitcast(BF16)) for b in self.G]
        self.gi = self.ti = self.si = 0

    def tf(self):
        self.fi = (self.fi + 1) % len(self.fpool)
        return self.fpool[self.fi]

    def tb(self):
        self.bi = (self.bi + 1) % len(self.bpool)
        return self.bpool[self.bi]

    def gb(self):
        self.gi = (self.gi + 1) % len(self.G)
        return self.G[self.gi]

    def tp(self):
        self.pi = (self.pi + 1) % len(self.ppool)
        return self.ppool[self.pi]

    def tbk(self):
        self.gi = (self.gi + 1) % len(self.G)
        return self.TB[self.gi]

    def st(self):
        self.stat_i = (self.stat_i + 1) % len(self.stat)
        return self.stat[self.stat_i]

    def rslot(self):
        self.ring_i = (self.ring_i + 1) % len(self.ring)
        return self.ring[self.ring_i]

    def mm(self, out, lhsT, rhs, start, stop, R, W):
        self.p.op("pe", lambda e: e.matmul(out, lhsT=lhsT, rhs=rhs, start=start, stop=stop), R, W)

    def tr(self, out, in_, ident, R, W):
        self.p.op("pe", lambda e: e.transpose(out=out, in_=in_, identity=ident), R, W)

    def act(self, out, in_, func, R, W, scale=1.0, bias=0.0, accum=None):
        if accum is None:
            self.p.op("act", lambda e: e.activation(out=out, in_=in_, func=func, bias=bias, scale=scale), R, W)
        else:
            self.p.op("act", lambda e: e.activation(out=out, in_=in_, func=func, bias=bias, scale=scale,
                                                    accum_out=accum), R, W)

    def ts(self, out, in0, s1, s2, op0, op1, R, W, eng="dve"):
        if s2 is None:
            self.p.op(eng, lambda e: e.tensor_scalar(out=out, in0=in0, scalar1=s1, scalar2=None, op0=op0), R, W)
        else:
            self.p.op(eng, lambda e: e.tensor_scalar(out=out, in0=in0, scalar1=s1, scalar2=s2, op0=op0, op1=op1), R, W)

    def stt(self, out, in0, sc, in1, op0, op1, R, W):
        self.p.op("dve", lambda e: e.scalar_tensor_tensor(out=out, in0=in0, scalar=sc, in1=in1, op0=op0, op1=op1), R, W)

    def tt(self, out, in0, in1, op, R, W, eng="dve"):
        self.p.op(eng, lambda e: e.tensor_tensor(out=out, in0=in0, in1=in1, op=op), R, W)

    def cp(self, out, in_, R, W, eng="dve"):
        self.p.op(eng, lambda e: e.tensor_copy(out=out, in_=in_), R, W)

    def rcp(self, out, in_, R, W):
        self.p.op("dve", lambda e: e.reciprocal(out=out, in_=in_), R, W)

    def ld(self, out, in_, R, W, q="sp", **kw):
        return self.p.dma(q, out, in_, reads=R, writes=W, **kw)

    def rstd(self, ssq, n, rows):
        pass

    def setup(self):
        d, p = self.d, self.p
        self.ld(self.ident_f.ap, d["ident"], [], [self.ident_f])
        self.cp(self.ident_b.ap, self.ident_f.ap, [self.ident_f], [self.ident_b])
        self.p.op("dve", lambda e: e.memset(self.ones_b.ap, 1.0), [], [self.ones_b])
        self.p.op("dve", lambda e: e.memset(self.ones_f.ap, 1.0), [], [self.ones_f])
        self.p.op("dve", lambda e: e.memset(self.ones_ff.ap, 1.0), [], [self.ones_ff])
        self.p.op("dve", lambda e: e.memset(self.eps.ap, EPS), [], [self.eps])
        self.p.op("dve", lambda e: e.memset(self.one.ap, 1.0), [], [self.one])
        self.ld(self.mrow_f.ap, d["mrow"], [], [self.mrow_f])
        self.cp(self.mrow.ap, self.mrow_f.ap, [self.mrow_f], [self.mrow])
        self.ld(self.sel_s.ap, d["sel_s"], [], [self.sel_s])
        self.p.op("dve", lambda e: e.memset(self.maskt.ap, 0.0), [], [self.maskt])
        self.cp(self.maskt.ap[0:1, :], self.mrow.ap[0:1, 0:640], [self.mrow], [self.maskt])
        self.p.op("dve", lambda e: e.memset(self.kpedup.ap, 0.0), [], [self.kpedup])
        for l in range(2):
            for name, src in (("Win", "w_in"), ("Wa", "w_branch_a"), ("Wb", "w_branch_b"), ("Wo", "w_out")):
                self.ld(d[name][l].rearrange("k (a b) -> k a b", a=4), d[src][l].rearrange("k (a b) -> k a b", a=4),
                        [], [self.DB(f"{name}{l}"), self.DB("castchain")], q="pool", chan=f"cast_{name}{l}")
            src = d["w_q_up"][l].rearrange("k (h e) -> k h e", h=8)
            dst = d["Wq"][l]
            B = self.DB(f"Wq{l}")
            kw = dict(q="pool")
            B = self.DB(f"Wq{l}")
            self.ld(dst[:, 0:1024].rearrange("k (h e) -> k h e", h=8), src[:, :, 0:128], [], [B, self.DB("castchain")], chan=f"cast_Wq{l}a", **kw)
            self.ld(dst[:, 1024:1536].rearrange("k (h e) -> k h e", h=8), src[:, :, 128:192], [], [B, self.DB("castchain")], chan=f"cast_Wq{l}b", **kw)
            sw = dst[:, 1536:2048].rearrange("k (h e) -> k h e", h=8)
            self.ld(sw[:, :, 0:32], src[:, :, 160:192], [], [B, self.DB("castchain")], chan=f"cast_Wq{l}c", **kw)
            self.ld(sw[:, :, 32:64], src[:, :, 128:160], [], [B, self.DB("castchain")], chan=f"cast_Wq{l}d", **kw)
            self.ld(dst[:, 2048:2112], d["w_q_up"][l][:, 128:192], [], [B, self.DB("castchain")], chan=f"cast_Wq{l}e", **kw)

    def wload(self, name, l, rows, c0, ncols):
        slot = self.rslot()
        kc = rows // 128
        view = slot.ap[:, 0:kc * ncols].rearrange("p (k e) -> p k e", k=kc)
        src = self.d[name][l].rearrange("(k p) e -> p k e", p=128)[:, :, c0:c0 + ncols]
        self.ld(view, src, [self.DB(f"{name}{l}")], [slot])
        return slot, view

    def run_pass(self, l, kind, seq):
        d, p, v = self.d, self.p, self.vec
        g = self.g = type("G", (), {})()
        g.l, g.kind, g.seq = l, kind, seq
        if kind == "p":
            g.T, g.nt, g.nseg, g.L, g.nsub, g.tsz = 512, self.SEQ // 512, 1, 512, 4, 128
            g.c_src = d["cp"][seq:seq + 1, :]
            g.sel = self.ones_f.ap[0:1, 0:128]
        else:
            g.T, g.nt, g.nseg, g.L, g.nsub, g.tsz = 64, 1, 4, 16, 1, 64
            g.c_src = d["cs"]
            g.sel = self.sel_s.ap
        ns = g.nseg
        NC = dict(allow_slow_non_contiguous=True)
        self.p.tag = "PASS"
        self.ld(self.wuv.ap, d["w_uv"][l].rearrange("(rc p) e -> p rc e", p=128), [], [self.wuv], q="pool")
        self.ld(self.wa.ap, d["lru_wa"][l].rearrange("n k j -> k n j"), [], [self.wa], q="pool")
        self.ld(self.wx.ap, d["lru_wx"][l].rearrange("n k j -> k n j"), [], [self.wx], q="pool")
        slot = self.rslot()
        wukf = slot.ap.bitcast(F32).rearrange("p (rc e) -> p rc e", rc=2)
        self.ld(wukf, d["w_uk"][l].rearrange("(rc p) e -> p rc e", p=128), [], [slot])
        for h in range(8):
            G = self.gb()
            for rc in range(2):
                self.tr(G.ap[:, rc * 128:(rc + 1) * 128], wukf[:, rc, h * 128:(h + 1) * 128], self.ident_f.ap, [slot, self.ident_f], [G])
            self.cp(self.wukT.ap[:, h, :], G.ap[:, 0:256], [G], [self.wukT])
        for k in range(4):
            self.ld(self.cw.ap[:, :, k], d["conv_w"][l, k].rearrange("(c p) -> p c", p=128), [], [self.cw], **NC)
        for n in ("conv_b", "lru_ba", "lru_bx", "lru_lambda", "pre_norm", "post_norm"):
            self.ld(v[n].ap, d[n][l].rearrange("(c p) -> p c", p=128), [], [v[n]], **NC)
        self.ld(self.qg.ap, d["q_norm"][l].rearrange("(c p) -> p c", p=128), [], [self.qg], **NC)
        self.ld(self.adab.ap, d["ada_b"][l].rearrange("(c p) -> p c", p=128), [], [self.adab], **NC)
        self.ld(self.kvg.ap, d["kv_norm"][l].partition_broadcast(128), [], [self.kvg])
        self.ts(v["hba"].ap, v["lru_ba"].ap, 0.5, None, ALU.mult, None, [v["lru_ba"]], [v["hba"]])
        self.ts(v["hbx"].ap, v["lru_bx"].ap, 0.5, None, ALU.mult, None, [v["lru_bx"]], [v["hbx"]])
        self.act(v["sp"].ap, v["lru_lambda"].ap, AF.Exp, [v["lru_lambda"]], [v["sp"]], scale=-1.0)
        self.act(v["sp"].ap, v["sp"].ap, AF.Ln, [v["sp"]], [v["sp"]], bias=1.0)
        self.ts(v["sp8"].ap, v["sp"].ap, -8.0, None, ALU.mult, None, [v["sp"]], [v["sp8"]])
        self.ts(v["hsp"].ap, v["sp"].ap, -4.0, None, ALU.mult, None, [v["sp"]], [v["hsp"]])
        for c in range(8):
            for k in range(4):
                self.ts(self.diag.ap[:, c, k, :], self.ident_f.ap, self.cw.ap[:, c, k:k + 1], None, ALU.mult, None,
                        [self.ident_f, self.cw], [self.diag])
        self.ld(self.c_sb.ap[0:ns, :], g.c_src, [], [self.c_sb])
        self.act(self.c_t.ap[0:ns, :], self.c_sb.ap[0:ns, :], AF.Tanh, [self.c_sb], [self.c_t], scale=0.5)
        self.ts(self.c_t.ap[0:ns, :], self.c_t.ap[0:ns, :], 0.5, 0.5, ALU.mult, ALU.add, [self.c_t], [self.c_t])
        self.tt(self.c_t.ap[0:ns, :], self.c_t.ap[0:ns, :], self.c_sb.ap[0:ns, :], ALU.mult, [self.c_t, self.c_sb], [self.c_t])
        for c in range(8):
            G = self.gb()
            self.tr(G.ap[:, 0:ns], self.c_t.ap[0:ns, c * 128:(c + 1) * 128], self.ident_f.ap[0:ns, 0:ns], [self.c_t, self.ident_f], [G])
            self.cp(self.scT.ap[:, c, 0:ns], G.ap[:, 0:ns], [G], [self.scT])
        for e in range(24):
            slot = self.rslot()
            wv = slot.ap.bitcast(F32)[:, 0:1024].rearrange("p (k e) -> p k e", k=8)
            self.ld(wv, d["ada_w"][l].rearrange("(k p) e -> p k e", p=128)[:, :, e * 128:(e + 1) * 128], [], [slot])
            G = self.gb()
            for kc in range(8):
                self.mm(G.ap[:, 0:ns], wv[:, kc, :], self.scT.ap[:, kc, 0:ns], kc == 0, kc == 7, [slot, self.scT], [G])
            self.ts(self.modT.ap[:, e, 0:ns], G.ap[:, 0:ns], self.adab.ap[:, e:e + 1], None, ALU.add, None, [G, self.adab], [self.modT])
        for c in range(8):
            self.ts(self.gs.ap[:, c, 0:ns], self.modT.ap[:, 8 + c, 0:ns], 1.0, v["pre_norm"].ap[:, c:c + 1], ALU.add, ALU.mult,
                    [self.modT, v["pre_norm"]], [self.gs])
            self.ts(self.ggT.ap[:, c, 0:ns], self.modT.ap[:, 16 + c, 0:ns], v["post_norm"].ap[:, c:c + 1], None, ALU.mult, None,
                    [self.modT, v["post_norm"]], [self.ggT])
        for half in range(2):
            G = self.gb()
            for cc in range(4):
                c = half * 4 + cc
                self.tr(G.ap[0:ns, cc * 128:(cc + 1) * 128], self.ggT.ap[:, c, 0:ns], self.ident_f.ap, [self.ggT, self.ident_f], [G])
            self.cp(self.gg_tm.ap[0:ns, half * 512:(half + 1) * 512], G.ap[0:ns, :], [G], [self.gg_tm])
        for half in range(2):
            G = self.gb()
            self.mm(G.ap[0:g.tsz, :], g.sel, self.gg_tm.ap[0:ns, half * 512:(half + 1) * 512], True, True, [self.gg_tm, self.sel_s, self.ones_f], [G])
            self.cp(self.GG.ap[0:g.tsz, half * 512:(half + 1) * 512], G.ap[0:g.tsz, :], [G], [self.GG])
        if kind == "p":
            self.p.op("dve", lambda e: e.memset(self.hcar.ap, 0.0), [], [self.hcar])
            self.p.op("dve", lambda e: e.memset(self.carry.ap, 0.0), [], [self.carry])
        else:
            for g_ in range(4):
                self.ld(self.hcar.ap[:, :, g_], d["slru"][l, g_].rearrange("(c p) -> p c", p=128), [], [self.hcar], **NC)
                for k in range(3):
                    self.ld(self.carry_f.ap[:, :, g_, k], d["sconv"][l, g_, k].rearrange("(c p) -> p c", p=128), [], [self.carry_f], **NC)
            self.cp(self.carry.ap[:, :, :, 0:3], self.carry_f.ap, [self.carry_f], [self.carry])
        self.chk("pass_setup")
        for t in range(g.nt):
            self.tile(t)
            self.chk("tile")
        if kind == "p":
            for k in range(3):
                self.ld(d["nconv_p"][l, seq, k].rearrange("(c p) -> p c", p=128), self.convlast.ap[:, :, 0, k],
                        [self.convlast], [self.DB("o_nconv")], q="act", final=True, **NC)
            self.ld(d["nlru_p"][l, seq].rearrange("(c p) -> p c", p=128), self.hcar.ap[:, :, 0],
                    [self.hcar], [self.DB("o_nlru")], q="act", final=True, **NC)
        else:
            for g_ in range(4):
                for k in range(3):
                    self.ld(d["nconv_s"][l, g_, k].rearrange("(c p) -> p c", p=128), self.convlast.ap[:, :, g_, k],
                            [self.convlast], [self.DB("o_nconv")], q="act", final=True, **NC)
                self.ld(d["nlru_s"][l, g_].rearrange("(c p) -> p c", p=128), self.hcar.ap[:, :, g_],
                        [self.hcar], [self.DB("o_nlru")], q="act", final=True, **NC)

    def stageA_gen(self, t):
        g, d = self.g, self.d
        l, T, ns, L, nsub, tsz, seq = g.l, g.T, g.nseg, g.L, g.nsub, g.tsz, g.seq
        prompt = g.kind == "p"
        if prompt:
            xsrc = (d["xp"] if l == 0 else d["x1p"])[seq]
            row0 = t * 512
            xn_ = lambda s: f"x1p_{seq}_{t}_{s}"
        else:
            xsrc = d["xs"] if l == 0 else d["x1s"]
            row0 = 0
            xn_ = lambda s: "x1s"
        hT, hTc = self.hT, self.hTc
        self.p.tag = "A"
        if prompt:
            self.ld(self.cosT.ap, d["cosT"][:, row0:row0 + 512], [], [self.cosT])
            self.ld(self.ssinT.ap, d["ssinT"][:, row0:row0 + 512], [], [self.ssinT])
            self.ld(self.cos_tm.ap, d["cos_tm"][row0:row0 + 512, :].rearrange("(s p) e -> p s e", p=128), [], [self.cos_tm])
            self.ld(self.ssin_tm.ap, d["ssin_tm"][row0:row0 + 512, :].rearrange("(s p) e -> p s e", p=128), [], [self.ssin_tm])
        else:
            self.ld(self.cosT.ap[:, 0:64], d["cosT_s"], [], [self.cosT])
            self.ld(self.ssinT.ap[:, 0:64], d["ssinT_s"], [], [self.ssinT])
            self.ld(self.cos_tm.ap[0:64, 0, :], d["cos_tm_s"], [], [self.cos_tm])
            self.ld(self.ssin_tm.ap[0:64, 0, :], d["ssin_tm_s"], [], [self.ssin_tm])
        slot, wkv = self.wload("Win", l, 1024, OFF["ckv"], 320)
        for s in range(nsub):
            xt, xn = self.xt[s % 2], self.xn[s % 2]
            r0 = row0 + s * 128
            self.ld(xt.ap[0:tsz, :], xsrc[r0:r0 + tsz, :], [self.DB(xn_(s))] if l == 1 else [], [xt])
            st = self.st()
            self.act(xn.ap[0:tsz, :], xt.ap[0:tsz, :], AF.Square, [xt], [xn, st], accum=st.ap[0:tsz, 0:1])
            self.chk("A1")
            self.act(st.ap[0:tsz, 1:2], st.ap[0:tsz, 0:1], AF.Sqrt, [st], [st], scale=1.0 / D, bias=self.eps.ap[0:tsz, :])
            self.chk("A2")
            self.rcp(st.ap[0:tsz, 2:3], st.ap[0:tsz, 1:2], [st], [st])
            self.act(xn.ap[0:tsz, :], xt.ap[0:tsz, :], AF.Identity, [xt, st], [xn], scale=st.ap[0:tsz, 2:3])
            self.chk("A3")
            for half in range(2):
                TBk = self.tbk()
                for cc in range(4):
                    c = half * 4 + cc
                    self.tr(TBk.ap[:, cc * 128:cc * 128 + tsz], xn.ap[0:tsz, c * 128:(c + 1) * 128],
                            self.ident_b.ap[0:tsz, 0:tsz], [xn, self.ident_b], [TBk])
                self.chk("A4")
                for cc in range(4):
                    c = half * 4 + cc
                    for sg in range(ns):
                        c0, n = (0, tsz) if prompt else (sg * L, L)
                        self.act(hT.ap[:, c, s * 128 + c0:s * 128 + c0 + n], TBk.ap[:, cc * 128 + c0:cc * 128 + c0 + n],
                                 AF.Identity, [TBk, self.gs, self.modT], [hTc[c]],
                                 scale=self.gs.ap[:, c, sg:sg + 1], bias=self.modT.ap[:, c, sg:sg + 1])
            self.p.tag = "B1"
            G = self.gb()
            for kc in range(8):
                self.mm(G.ap[0:tsz, 0:320], hT.ap[:, kc, s * 128:s * 128 + tsz], wkv[:, kc, :], kc == 0, kc == 7, [hTc[kc], slot], [G])
            st, junk = self.st(), self.tb()
            self.act(junk.ap[0:tsz, 0:256], G.ap[0:tsz, 0:256], AF.Square, [G], [junk, st], accum=st.ap[0:tsz, 0:1])
            self.act(st.ap[0:tsz, 1:2], st.ap[0:tsz, 0:1], AF.Sqrt, [st], [st], scale=1.0 / 256, bias=self.eps.ap[0:tsz, :])
            self.rcp(st.ap[0:tsz, 2:3], st.ap[0:tsz, 1:2], [st], [st])
            cst, kst = self.ckv_st[s % 2], self.kpe_st[s % 2]
            self.stt(cst.ap[0:tsz, :], G.ap[0:tsz, 0:256], st.ap[0:tsz, 2:3], self.kvg.ap[0:tsz, :], ALU.mult, ALU.mult, [G, st, self.kvg], [cst])
            kt = (t * 4 + s) if prompt else 16
            KM, KT, KP = self.K_tm[kt], self.K_T[kt], self.K_pe[kt]
            self.act(KM.ap[0:tsz, :], cst.ap[0:tsz, :], AF.Copy, [cst], [KM])
            r0 = row0 + s * 128
            if prompt:
                self.ld(d["nckv_p"][l, seq, r0:r0 + tsz, :], cst.ap[0:tsz, :], [cst], [self.DB("o_ckv")], q="act", final=True)
            else:
                self.ld(d["nckv_s"][l], cst.ap[0:tsz, :], [cst], [self.DB("o_ckv")], q="act", final=True)
            t1 = self.tf()
            cosm, sinm = self.cos_tm.ap[0:tsz, s, :], self.ssin_tm.ap[0:tsz, s, :]
            self.tt(kst.ap[0:tsz, :], G.ap[0:tsz, 256:320], cosm, ALU.mult, [G, self.cos_tm], [kst])
            self.tt(t1.ap[0:tsz, 0:32], G.ap[0:tsz, 288:320], sinm[:, 0:32], ALU.mult, [G, self.ssin_tm], [t1])
            self.tt(t1.ap[0:tsz, 32:64], G.ap[0:tsz, 256:288], sinm[:, 32:64], ALU.mult, [G, self.ssin_tm], [t1])
            self.tt(kst.ap[0:tsz, :], kst.ap[0:tsz, :], t1.ap[0:tsz, 0:64], ALU.add, [kst, t1], [kst])
            if prompt:
                self.ld(d["nkpe_p"][l, seq, r0:r0 + tsz, :], kst.ap[0:tsz, :], [kst], [self.DB("o_kpe")], q="act", final=True)
            else:
                self.ld(d["nkpe_s"][l], kst.ap[0:tsz, :], [kst], [self.DB("o_kpe")], q="act", final=True)
            self.act(self.kpedup.ap[0:tsz, 0:64], kst.ap[0:tsz, :], AF.Copy, [kst], [self.kpedup])
            TBk = self.tbk()
            idb = self.ident_b.ap[0:tsz, 0:tsz]
            self.tr(TBk.ap[:, 0:tsz], KM.ap[0:tsz, 0:128], idb, [KM, self.ident_b], [TBk])
            self.tr(TBk.ap[:, 128:128 + tsz], KM.ap[0:tsz, 128:256], idb, [KM, self.ident_b], [TBk])
            self.tr(TBk.ap[:, 256:256 + tsz], self.kpedup.ap[0:tsz, :], idb, [self.kpedup, self.ident_b], [TBk])
            self.act(KT.ap[:, 0, 0:tsz], TBk.ap[:, 0:tsz], AF.Copy, [TBk], [KT])
            self.act(KT.ap[:, 1, 0:tsz], TBk.ap[:, 128:128 + tsz], AF.Copy, [TBk], [KT])
            self.act(KP.ap[:, 0:tsz], TBk.ap[:, 256:256 + tsz], AF.Copy, [TBk], [KP])
            yield
            self.p.tag = "A"

    def tile(self, t):
        g, d, v = self.g, self.d, self.vec
        l, T, ns, L, nsub, tsz, seq = g.l, g.T, g.nseg, g.L, g.nsub, g.tsz, g.seq
        prompt = g.kind == "p"
        last_tile = t == g.nt - 1
        if prompt:
            xsrc = (d["xp"] if l == 0 else d["x1p"])[seq]
            ydst = (d["x1p"] if l == 0 else d["yp"])[seq]
            row0 = t * 512
            xn_ = lambda s: f"x1p_{seq}_{t}_{s}"
        else:
            xsrc = d["xs"] if l == 0 else d["x1s"]
            ydst = d["x1s"] if l == 0 else d["ys"]
            row0 = 0
            xn_ = lambda s: "x1s"
        hT, hTc = self.hT, self.hTc
        self.chk("A0")
        self.p.tag = "A"
        if getattr(self, "stageA_done", None) != (id(g), t):
            for _ in self.stageA_gen(t):
                pass
        self.chk("A")
        self.chk("B1")
        self.p.tag = "B2"
        sA, wA = self.wload("Win", l, 1024, OFF["cq"], 384)
        sB, wB = self.wload("Win", l, 1024, OFF["cq"] + 384, 384)
        cq = self.cq
        SS = self.O[2]
        for e in range(6):
            sl, w = (sA, wA) if e < 3 else (sB, wB)
            G = self.gb()
            for kc in range(8):
                self.mm(G.ap[:, 0:T], w[:, kc, (e % 3) * 128:(e % 3 + 1) * 128], hT.ap[:, kc, 0:T], kc == 0, kc == 7, [sl, hTc[kc]], [G])
            sq = self.tb()
            self.act(sq.ap[:, 0:T], G.ap[:, 0:T], AF.Square, [G], [sq])
            self.act(cq[e].ap[:, 0:T], G.ap[:, 0:T], AF.Copy, [G], [cq[e]])
            self.mm(SS.ap[:, 0:T], self.ones_b.ap, sq.ap[:, 0:T], e == 0, e == 5, [sq, self.ones_b], [SS])
        rq = self.tf()
        self.act(rq.ap[:, 0:T], SS.ap[:, 0:T], AF.Sqrt, [SS], [rq], scale=1.0 / CQ, bias=self.eps.ap)
        self.rcp(rq.ap[:, 0:T], rq.ap[:, 0:T], [rq], [rq])
        for e in range(6):
            self.stt(cq[e].ap[:, 0:T], cq[e].ap[:, 0:T], self.qg.ap[:, e:e + 1], rq.ap[:, 0:T], ALU.mult, ALU.mult, [cq[e], self.qg, rq], [cq[e]])
        self.chk("B2")
        xabv = self.xab.ap[:, 0:ns * (L + 3)].rearrange("p (g l) -> p g l", g=ns)
        HS = {}

        def P1q_parts(h):
            self.p.tag = "P1"
            c = h
            H = HS.setdefault(h, {})
            sQ = self.rslot()
            wq = sQ.ap[:, 0:3 * 768].rearrange("p (a k e) -> p a k e", a=3, k=6)
            wqsrc = d["Wq"][l].rearrange("(k p) e -> p k e", p=128)
            self.ld(wq[:, 0], wqsrc[:, :, 128 * h:128 * h + 128], [self.DB(f"Wq{l}")], [sQ])
            self.ld(wq[:, 1], wqsrc[:, :, 1024 + 64 * h:1024 + 64 * h + 128], [self.DB(f"Wq{l}")], [sQ])
            self.ld(wq[:, 2], wqsrc[:, :, 1536 + 64 * h:1536 + 64 * h + 128], [self.DB(f"Wq{l}")], [sQ])
            if True:
                GA, GB = self.gb(), None
                for kc in range(6):
                    self.mm(GA.ap[:, 0:T], wq[:, 1, kc, :], cq[kc].ap[:, 0:T], kc == 0, kc == 5, [sQ, cq[kc]], [GA])
                t1, t2 = self.tf(), self.tf()
                self.tt(t1.ap[:, 0:T], GA.ap[:, 0:T], self.cosT.ap[:, 0:T], ALU.mult, [GA, self.cosT], [t1])
                yield
                self.p.tag = "P1"
                GB = self.gb()
                for kc in range(6):
                    self.mm(GB.ap[:, 0:T], wq[:, 2, kc, :], cq[kc].ap[:, 0:T], kc == 0, kc == 5, [sQ, cq[kc]], [GB])
                self.tt(t2.ap[:, 0:T], GB.ap[:, 0:T], self.ssinT.ap[:, 0:T], ALU.mult, [GB, self.ssinT], [t2])
                self.tt(self.qpe[h].ap[:, 0:T], t1.ap[:, 0:T], t2.ap[:, 0:T], ALU.add, [t1, t2], [self.qpe[h]])
            yield
            self.p.tag = "P1"
            Gq = self.gb()
            for kc in range(6):
                self.mm(Gq.ap[:, 0:T], wq[:, 0, kc, :], cq[kc].ap[:, 0:T], kc == 0, kc == 5, [sQ, cq[kc]], [Gq])
            qn = self.tb()
            self.act(qn.ap[:, 0:T], Gq.ap[:, 0:T], AF.Copy, [Gq], [qn])
            ql = self.qlat_p[h % 2] if prompt else self.qlat_s[h]
            yield
            self.p.tag = "P1"
            for rc in range(2):
                Gl = self.gb()
                self.mm(Gl.ap[:, 0:T], self.wukT.ap[:, h, rc * 128:(rc + 1) * 128], qn.ap[:, 0:T], True, True, [self.wukT, qn], [Gl])
                if rc == 0:
                    self.act(ql.ap[:, rc, 0:T], Gl.ap[:, 0:T], AF.Copy, [Gl], [ql])
                else:
                    self.cp(ql.ap[:, rc, 0:T], Gl.ap[:, 0:T], [Gl], [ql])
            H.update(ql=ql)

        def P1x_parts(h):
            self.p.tag = "P1"
            c = h
            H = HS.setdefault(h, {})
            sL = self.rslot()
            wl = sL.ap[:, 0:3072].rearrange("p (a k e) -> p a k e", a=3, k=8)
            wisrc = d["Win"][l].rearrange("(k p) e -> p k e", p=128)
            for a, nm in enumerate(("xa", "ga", "gb")):
                self.ld(wl[:, a], wisrc[:, :, OFF[nm] + 128 * c:OFF[nm] + 128 * c + 128], [self.DB(f"Win{l}")], [sL])
            G = self.gb()
            for kc in range(8):
                self.mm(G.ap[:, 0:T], wl[:, 0, kc, :], hT.ap[:, kc, 0:T], kc == 0, kc == 7, [sL, hTc[kc]], [G])
            Gv = G.ap[:, 0:T].rearrange("p (g l) -> p g l", g=ns)
            self.cp(xabv[:, :, 0:3], self.carry.ap[:, c, 0:ns, 0:3], [self.carry], [self.xab])
            self.act(xabv[:, :, 3:3 + L], Gv, AF.Copy, [G], [self.xab])
            self.cp(self.carry.ap[:, c, 0:ns, 0:3], xabv[:, :, L:L + 3], [self.xab], [self.carry])
            if last_tile:
                self.cp(self.convlast.ap[:, c, 0:ns, :], Gv[:, :, L - 3:L], [G], [self.convlast])
            yield
            self.p.tag = "P1"
            G2 = self.gb()
            for kc in range(8):
                self.mm(G2.ap[:, 0:T], wl[:, 1, kc, :], hT.ap[:, kc, 0:T], kc == 0, kc == 7, [sL, hTc[kc]], [G2])
            tg, sg = self.tb(), self.tb()
            self.act(tg.ap[:, 0:T], G2.ap[:, 0:T], AF.Tanh, [G2], [tg], scale=0.5)
            self.stt(sg.ap[:, 0:T], tg.ap[:, 0:T], 1.0, G2.ap[:, 0:T], ALU.add, ALU.mult, [tg, G2], [sg])
            H.update(sg=sg, wl=wl, sL=sL)

        def P1(h):
            for _ in P1q_parts(h):
                pass
            for _ in P1x_parts(h):
                pass

        def P1g(h):
            self.p.tag = "P1"
            c, H = h, HS[h]
            wl, sL = H["wl"], H["sL"]
            Gg = self.gb()
            for kc in range(8):
                self.mm(Gg.ap[:, 0:T], wl[:, 2, kc, :], hT.ap[:, kc, 0:T], kc == 0, kc == 7, [sL, hTc[kc]], [Gg])
            tgb = self.tb()
            self.act(tgb.ap[:, 0:T], Gg.ap[:, 0:T], AF.Tanh, [Gg], [tgb], scale=0.5)
            if prompt:
                sgb = self.sgbp[h % 2]
                self.stt(sgb.ap[:, 0:T], tgb.ap[:, 0:T], 1.0, Gg.ap[:, 0:T], ALU.add, ALU.mult, [tgb, Gg], [sgb])
            else:
                self.stt(self.sgb_s.ap[:, h, :], tgb.ap[:, 0:T], 1.0, Gg.ap[:, 0:T], ALU.add, ALU.mult, [tgb, Gg], [self.sgb_s])
            H.update(sgb=sgb if prompt else None)

        def P2(h):
            self.p.tag = "P2"
            c, H = h, HS[h]
            G3 = self.gb()
            for sgm in range(ns):
                for k in range(4):
                    self.mm(G3.ap[:, sgm * L:(sgm + 1) * L], self.diag.ap[:, c, k, :], xabv[:, sgm, k:k + L], k == 0, k == 3, [self.diag, self.xab], [G3])
            xcf, xcb = self.tf(), self.tb()
            self.act(xcf.ap[:, 0:T], G3.ap[:, 0:T], AF.Identity, [G3, v["conv_b"]], [xcf], bias=v["conv_b"].ap[:, c:c + 1])
            self.act(xcb.ap[:, 0:T], G3.ap[:, 0:T], AF.Identity, [G3, v["conv_b"]], [xcb], bias=v["conv_b"].ap[:, c:c + 1])
            H.update(xcf=xcf, xcb=xcb)

        def P3parts(h):
            self.p.tag = "P3"
            c, H = h, HS[h]
            xcf, xcb, sg = H["xcf"], H["xcb"], H["sg"]
            Gr = self.gb()
            self.mm(Gr.ap[:, 0:T], self.wa.ap[:, c, :], xcb.ap[:, 0:T], True, True, [self.wa, xcb], [Gr])
            Gi = self.gb()
            self.mm(Gi.ap[:, 0:T], self.wx.ap[:, c, :], xcb.ap[:, 0:T], True, True, [self.wx, xcb], [Gi])
            tr_, a_, a2, u = self.tf(), self.tf(), self.tf(), self.tf()
            ti = self.tb()
            self.act(tr_.ap[:, 0:T], Gr.ap[:, 0:T], AF.Tanh, [Gr, v["hba"]], [tr_], scale=0.5, bias=v["hba"].ap[:, c:c + 1])
            self.act(ti.ap[:, 0:T], Gi.ap[:, 0:T], AF.Tanh, [Gi, v["hbx"]], [ti], scale=0.5, bias=v["hbx"].ap[:, c:c + 1])
            yield
            self.p.tag = "P3"
            self.act(a_.ap[:, 0:T], tr_.ap[:, 0:T], AF.Exp, [tr_, v["hsp"]], [a_], scale=v["hsp"].ap[:, c:c + 1], bias=v["hsp"].ap[:, c:c + 1])
            self.act(a2.ap[:, 0:T], tr_.ap[:, 0:T], AF.Exp, [tr_, v["sp8"]], [a2], scale=v["sp8"].ap[:, c:c + 1], bias=v["sp8"].ap[:, c:c + 1])
            self.stt(u.ap[:, 0:T], ti.ap[:, 0:T], 1.0, xcf.ap[:, 0:T], ALU.add, ALU.mult, [ti, xcf], [u])
            yield
            self.p.tag = "P3"
            self.act(a2.ap[:, 0:T], a2.ap[:, 0:T], AF.Sqrt, [a2], [a2], scale=-1.0, bias=self.one.ap)
            self.stt(u.ap[:, 0:T], u.ap[:, 0:T], 0.5, a2.ap[:, 0:T], ALU.mult, ALU.mult, [u, a2], [u])
            hs = self.tf()
            for sgm in range(ns):
                cs_ = slice(sgm * L, (sgm + 1) * L)
                hc = self.hcar.ap[:, c, sgm:sgm + 1]
                self.p.op("dve", (lambda o, a0, b0, i0: (lambda e: e.tensor_tensor_scan(out=o, data0=a0, data1=b0, initial=i0, op0=ALU.mult, op1=ALU.add)))(
                    hs.ap[:, cs_], a_.ap[:, cs_], u.ap[:, cs_], hc), [a_, u, self.hcar], [hs])
                self.cp(hc, hs.ap[:, (sgm + 1) * L - 1:(sgm + 1) * L], [hs], [self.hcar])
            self.stt(self.ya[c].ap[:, 0:T], hs.ap[:, 0:T], 0.5, sg.ap[:, 0:T], ALU.mult, ALU.mult, [hs, sg], [self.ya[c]])

        def P3(h):
            for _ in P3parts(h):
                pass

        def UV(h):
            self.p.tag = "UV"
            sgb = HS[h]["sgb"]
            Gu = self.gb()
            for rc in range(2):
                self.mm(Gu.ap[:, 0:T], self.wuv.ap[:, rc, h * 128:(h + 1) * 128], self.olat.ap[:, rc, 0:T], rc == 0, rc == 1, [self.wuv, self.olat], [Gu])
            self.stt(self.yb[h].ap[:, 0:T], Gu.ap[:, 0:T], 0.5, sgb.ap[:, 0:T], ALU.mult, ALU.mult, [Gu, sgb], [self.yb[h]])

        if prompt:
            nk = 4 * t + 4
            fin = {}
            for _ in P1q_parts(0):
                pass
            for h in range(8):
                hooks = {}

                def add(kt, fn):
                    hooks.setdefault(kt, []).append(fn)
                uv_kt = (8 * nk) // 13
                if h >= 1:
                    fg = fin[h - 1]()
                    fstep = lambda g_=fg: next(g_, None)
                    for i in range(4):
                        add(min(1 + i, max(uv_kt - 1, 1)), fstep)
                gx = P1x_parts(h)
                stepx = lambda g_=gx: next(g_, None)
                g3 = P3parts(h)
                step3 = lambda g_=g3: next(g_, None)
                if h < 7:
                    gq = P1q_parts(h + 1)
                    stepq = lambda g_=gq: next(g_, None)
                else:
                    stepq = None
                parts = [stepx, stepq, stepx, stepq, stepq, stepq, None, (lambda hh=h: P2(hh)), None, step3, step3, step3]
                for i, fn in enumerate(parts):
                    kt = (i * nk) // 13 if i < 8 else ((i + 1) * nk) // 13
                    if i == 6:
                        if h >= 1:
                            add(uv_kt, lambda hh=h: UV(hh - 1))
                        add(uv_kt, lambda hh=h: P1g(hh))
                    elif fn is not None:
                        add(kt, fn)
                fin[h] = self.attn_prompt(t, h, HS[h]["ql"], hooks)
            for _ in fin[7]():
                pass
            UV(7)
        else:
            for h in range(8):
                P1(h); P1g(h); P2(h); P3(h)
        self.chk("heads")
        if not prompt:
            self.attn_sample()
        self.p.tag = "MERGE"
        mg = self.mg
        for c in range(8):
            sM = self.rslot()
            wm = sM.ap.rearrange("p (a k e) -> p a k e", a=4, k=8)
            for a, (nm, c0) in enumerate((("Wa", 128 * c), ("Wb", 128 * c), ("Win", OFF["ua"] + 128 * c), ("Win", OFF["ub"] + 128 * c))):
                self.ld(wm[:, a], d[nm][l].rearrange("(k p) e -> p k e", p=128)[:, :, c0:c0 + 128], [self.DB(f"{nm}{l}")], [sM])
            Gua = self.gb()
            for kc in range(8):
                self.mm(Gua.ap[:, 0:T], wm[:, 2, kc, :], hT.ap[:, kc, 0:T], kc == 0, kc == 7, [sM, hTc[kc]], [Gua])
            tua, tub = self.tb(), self.tb()
            self.act(tua.ap[:, 0:T], Gua.ap[:, 0:T], AF.Tanh, [Gua], [tua], scale=0.5)
            Gub = self.gb()
            for kc in range(8):
                self.mm(Gub.ap[:, 0:T], wm[:, 3, kc, :], hT.ap[:, kc, 0:T], kc == 0, kc == 7, [sM, hTc[kc]], [Gub])
            self.act(tub.ap[:, 0:T], Gub.ap[:, 0:T], AF.Tanh, [Gub], [tub], scale=0.5)
            m1, m2 = self.tf(), self.tf()
            GA = self.gb()
            for kc in range(8):
                self.mm(GA.ap[:, 0:T], wm[:, 0, kc, :], self.ya[kc].ap[:, 0:T], kc == 0, kc == 7, [sM, self.ya[kc]], [GA])
            self.stt(m1.ap[:, 0:T], tua.ap[:, 0:T], 1.0, GA.ap[:, 0:T], ALU.add, ALU.mult, [tua, GA], [m1])
            GB = self.gb()
            for kc in range(8):
                self.mm(GB.ap[:, 0:T], wm[:, 1, kc, :], self.yb[kc].ap[:, 0:T], kc == 0, kc == 7, [sM, self.yb[kc]], [GB])
            self.stt(m2.ap[:, 0:T], tub.ap[:, 0:T], 1.0, GB.ap[:, 0:T], ALU.add, ALU.mult, [tub, GB], [m2])
            self.tt(m1.ap[:, 0:T], m1.ap[:, 0:T], m2.ap[:, 0:T], ALU.add, [m1, m2], [m1])
            self.act(mg[c].ap[:, 0:T], m1.ap[:, 0:T], AF.Copy, [m1], [mg[c]], scale=0.5)
        self.chk("merge")
        self.p.tag = "OUT"
        so0, wo0 = self.wload("Wo", l, 1024, 0, 512)
        so1, wo1 = self.wload("Wo", l, 1024, 512, 512)
        genA = None
        if prompt and t < g.nt - 1:
            genA = self.stageA_gen(t + 1)
        for s in range(nsub):
            if genA is not None:
                next(genA, None)
                self.p.tag = "OUT"
            xt = self.xt[s % 2]
            r0 = row0 + s * 128
            self.ld(xt.ap[0:tsz, :], xsrc[r0:r0 + tsz, :], [self.DB(xn_(s))] if l == 1 else [], [xt])
            Gs = []
            for half, (so, wo) in enumerate(((so0, wo0), (so1, wo1))):
                G = self.gb()
                for kc in range(8):
                    self.mm(G.ap[0:tsz, :], mg[kc].ap[:, s * 128:s * 128 + tsz], wo[:, kc, :], kc == 0, kc == 7, [mg[kc], so], [G])
                Gs.append(G)
            st, junk = self.st(), self.tb()
            self.act(junk.ap[0:tsz, :], Gs[0].ap[0:tsz, :], AF.Square, [Gs[0]], [junk, st], accum=st.ap[0:tsz, 0:1])
            self.act(junk.ap[0:tsz, :], Gs[1].ap[0:tsz, :], AF.Square, [Gs[1]], [junk, st], accum=st.ap[0:tsz, 1:2])
            self.tt(st.ap[0:tsz, 0:1], st.ap[0:tsz, 0:1], st.ap[0:tsz, 1:2], ALU.add, [st], [st])
            self.act(st.ap[0:tsz, 1:2], st.ap[0:tsz, 0:1], AF.Sqrt, [st], [st], scale=1.0 / D, bias=self.eps.ap[0:tsz, :])
            self.rcp(st.ap[0:tsz, 2:3], st.ap[0:tsz, 1:2], [st], [st])
            for half in range(2):
                tmp = self.tf()
                hs_ = slice(half * 512, (half + 1) * 512)
                self.stt(tmp.ap[0:tsz, :], Gs[half].ap[0:tsz, :], st.ap[0:tsz, 2:3], self.GG.ap[0:tsz, hs_], ALU.mult, ALU.mult, [Gs[half], st, self.GG], [tmp])
                self.tt(xt.ap[0:tsz, hs_], xt.ap[0:tsz, hs_], tmp.ap[0:tsz, :], ALU.add, [xt, tmp], [xt])
            self.ld(ydst[r0:r0 + tsz, :], xt.ap[0:tsz, :], [xt], [self.DB(xn_(s)) if l == 0 else self.DB("o_y")], q="act", final=(l == 1))
        if genA is not None:
            for _ in genA:
                pass
            self.stageA_done = (id(g), t + 1)

    def attn_prompt(self, t, h, ql, hooks=None):
        hooks = hooks or {}
        nk = 4 * t + 4
        qpe = self.qpe[h]
        O, mk = self.O, self.maskt
        acc = self.accp[h % 2]

        def PV(kt, c0, P, KM):
            for rc in range(2):
                self.mm(O[rc].ap[:, c0:512], KM.ap[:, rc * 128:(rc + 1) * 128], P.ap[:, c0:512], kt == 0, kt == nk - 1, [KM, P], [O[rc]])

        self.p.tag = "ATT"
        pend = []
        Sb = [self.S[0], self.S[1], self.O[2]]
        for kt in range(nk):
            j = kt - 4 * t
            c0 = 128 * j if j > 0 else 0
            S = Sb[kt % 3]
            KT, KP, KM = self.K_T[kt], self.K_pe[kt], self.K_tm[kt]
            self.mm(S.ap[:, c0:512], KT.ap[:, 0, :], ql.ap[:, 0, c0:512], True, False, [KT, ql], [S])
            self.mm(S.ap[:, c0:512], KT.ap[:, 1, :], ql.ap[:, 1, c0:512], False, False, [KT, ql], [S])
            self.mm(S.ap[:, c0:512], KP.ap, qpe.ap[:, c0:512], False, j < 0, [KP, qpe], [S])
            if j >= 0:
                self.mm(S.ap[:, c0:c0 + 128], mk.ap[:, 0:128], mk.ap[:, 128:256], False, True, [mk], [S])
            P = self.tp()
            self.act(P.ap[:, c0:512], S.ap[:, c0:512], AF.Exp, [S], [P], scale=SCALE)
            fuse01 = nk >= 2 and (1 - 4 * t) <= 0
            if kt == 0:
                P0 = P
                if not fuse01:
                    self.cp(acc.ap, P.ap, [P], [acc], eng="pool")
            elif kt == 1 and fuse01:
                self.tt(acc.ap, P0.ap, P.ap, ALU.add, [P0, P], [acc], eng="pool")
            else:
                self.tt(acc.ap[:, c0:512], acc.ap[:, c0:512], P.ap[:, c0:512], ALU.add, [acc, P], [acc], eng="pool")
            pend.append((kt, c0, P, KM))
            if len(pend) > 2:
                PV(*pend.pop(0))
            for fn in hooks.get(kt, ()):
                fn()
                self.p.tag = "ATT"
        for pv in pend:
            PV(*pv)
        for rc in range(2):
            self.act(self.olat.ap[:, rc, :], O[rc].ap, AF.Copy, [O[rc]], [self.olat])

        def finalize():
            self.p.tag = "FIN"
            Gs = self.gb()
            self.mm(Gs.ap, self.ones_ff.ap, acc.ap, True, True, [self.ones_ff, acc], [Gs])
            rs = self.tf()
            self.cp(rs.ap, Gs.ap, [Gs], [rs])
            for q4 in range(4):
                self.rcp(rs.ap[:, q4 * 128:(q4 + 1) * 128], rs.ap[:, q4 * 128:(q4 + 1) * 128], [rs], [rs])
                if q4 < 3:
                    yield
                    self.p.tag = "FIN"
            for rc in range(2):
                self.tt(self.olat.ap[:, rc, :], self.olat.ap[:, rc, :], rs.ap, ALU.mult, [self.olat, rs], [self.olat])
        return finalize

    def attn_sample(self):
        self.p.tag = "ATTS"
        g, d = self.g, self.d
        l = g.l
        O, mk = self.O, self.maskt
        idb = self.ident_b.ap
        for sgm in range(4):
            self.ld(self.ckvtm_all.ap[:, 0:16, :], d["cckv"][l, sgm].rearrange("(j p) r -> p j r", p=128), [], self.K_tm[0:16], q="pool", chan="kc_ckv")
            slot = self.rslot()
            kst = slot.ap[:, 0:2048].rearrange("p (j e) -> p j e", j=16)
            self.ld(kst[:, :, 0:64], d["ckpe"][l, sgm].rearrange("(j p) e -> p j e", p=128), [], [slot], q="pool")
            self.p.op("dve", (lambda a: (lambda e: e.memset(a, 0.0)))(kst[:, :, 64:128]), [], [slot])
            for kt in range(16):
                KM, KT, KP = self.K_tm[kt], self.K_T[kt], self.K_pe[kt]
                TBk = self.tbk()
                self.tr(TBk.ap[:, 0:128], KM.ap[:, 0:128], idb, [KM, self.ident_b], [TBk])
                self.tr(TBk.ap[:, 128:256], KM.ap[:, 128:256], idb, [KM, self.ident_b], [TBk])
                self.tr(TBk.ap[:, 256:384], kst[:, kt, :], idb, [slot, self.ident_b], [TBk])
                self.act(KT.ap[:, 0, :], TBk.ap[:, 0:128], AF.Copy, [TBk], [KT])
                self.act(KT.ap[:, 1, :], TBk.ap[:, 128:256], AF.Copy, [TBk], [KT])
                self.act(KP.ap, TBk.ap[:, 256:384], AF.Copy, [TBk], [KP])
            cols = slice(sgm * 16, sgm * 16 + 16)

            def PV(kt, nk, P, KM):
                for rc in range(2):
                    self.mm(O[rc].ap[:, 0:128], KM.ap[0:nk, rc * 128:(rc + 1) * 128], P.ap[0:nk, 0:128], kt == 0, kt == 16, [KM, P], [O[rc]])
                self.mm(O[2].ap[:, 0:128], self.ones_b.ap[0:nk, :], P.ap[0:nk, 0:128], kt == 0, kt == 16, [self.ones_b, P], [O[2]])

            prev = None
            for kt in range(17):
                nk = 128 if kt < 16 else 64
                S = self.S[kt % 2]
                KM, KT, KP = self.K_tm[kt], self.K_T[kt], self.K_pe[kt]
                for h in range(8):
                    qpe, ql = self.qpe[h], self.qlat_s[h]
                    so = S.ap[0:nk, h * 16:(h + 1) * 16]
                    self.mm(so, KT.ap[:, 0, 0:nk], ql.ap[:, 0, cols], True, False, [KT, ql], [S])
                    self.mm(so, KT.ap[:, 1, 0:nk], ql.ap[:, 1, cols], False, False, [KT, ql], [S])
                    self.mm(so, KP.ap[:, 0:nk], qpe.ap[:, cols], False, kt < 16, [KP, qpe], [S])
                    if kt == 16:
                        self.mm(so, mk.ap[:, 384 + sgm * 64:384 + (sgm + 1) * 64], mk.ap[:, 256:272], False, True, [mk], [S])
                P = self.tp()
                self.act(P.ap[0:nk, 0:128], S.ap[0:nk, 0:128], AF.Exp, [S], [P], scale=SCALE)
                if prev is not None:
                    PV(*prev)
                prev = (kt, nk, P, KM)
            PV(*prev)
            rs = self.tf()
            self.rcp(rs.ap[:, 0:128], O[2].ap[:, 0:128], [O[2]], [rs])
            for rc in range(2):
                self.tt(self.olat.ap[:, rc, sgm * 128:(sgm + 1) * 128], O[rc].ap[:, 0:128], rs.ap[:, 0:128], ALU.mult, [O[rc], rs], [self.olat])
        for h in range(8):
            Gu = self.gb()
            for sgm in range(4):
                for rc in range(2):
                    self.mm(Gu.ap[:, sgm * 16:(sgm + 1) * 16], self.wuv.ap[:, rc, h * 128:(h + 1) * 128],
                            self.olat.ap[:, rc, sgm * 128 + h * 16:sgm * 128 + (h + 1) * 16], rc == 0, rc == 1, [self.wuv, self.olat], [Gu])
            self.stt(self.yb[h].ap[:, 0:64], Gu.ap[:, 0:64], 0.5, self.sgb_s.ap[:, h, :], ALU.mult, ALU.mult, [Gu, self.sgb_s], [self.yb[h]])


_CACHE = {}


def _tables():
    half = 32
    inv = (np.float32(10000.0) ** (-np.arange(half, dtype=np.float32) / np.float32(half))).astype(np.float32)
    pos = np.arange(4096, dtype=np.float32)
    ang = (pos[:, None] * inv[None, :]).astype(np.float32)
    cos, sin = np.cos(ang).astype(np.float32), np.sin(ang).astype(np.float32)
    cos_tm = np.concatenate([cos, cos], 1)
    ssin_tm = np.concatenate([-sin, sin], 1)
    cosT = np.ascontiguousarray(np.concatenate([cos_tm, cos_tm], 1).T)
    ssinT = np.ascontiguousarray(np.concatenate([ssin_tm, ssin_tm], 1).T)
    ps = np.tile(np.arange(2048, 2064), 4)
    mrow = np.zeros((1, 1024), np.float32)
    mrow[0, 64:128] = 1.0
    mrow[0, 128:192] = NEG
    mrow[0, 256:384] = NEG
    for g_ in range(4):
        u = np.ones(64, np.float32)
        u[g_ * 16:(g_ + 1) * 16] = 0.0
        mrow[0, 384 + g_ * 64:384 + (g_ + 1) * 64] = u
    sel = np.zeros((4, 64), np.float32)
    for g_ in range(4):
        sel[g_, g_ * 16:(g_ + 1) * 16] = 1.0
    return dict(ident=np.eye(128, dtype=np.float32), cosT=cosT, ssinT=ssinT, cos_tm=cos_tm, ssin_tm=ssin_tm,
                cosT_s=np.ascontiguousarray(cosT[:, ps]), ssinT_s=np.ascontiguousarray(ssinT[:, ps]),
                cos_tm_s=np.ascontiguousarray(cos_tm[ps]), ssin_tm_s=np.ascontiguousarray(ssin_tm[ps]),
                sel_s=sel, mrow=mrow)


def kernel(**inp):
    f = lambda a: np.ascontiguousarray(np.asarray(a, dtype=np.float32))
    if "mk" not in _CACHE:
        _CACHE["mk"] = MK()
    mk = _CACHE["mk"]
    tabs = _tables()
    wnames = ["ada_w", "ada_b", "pre_norm", "post_norm", "w_in", "conv_w", "conv_b", "lru_wa", "lru_ba", "lru_wx",
              "lru_bx", "lru_lambda", "q_norm", "w_q_up", "kv_norm", "w_branch_a", "w_branch_b", "w_out"]
    shared = {n: f(inp[n]) for n in wnames}
    shared["w_uk"] = f(inp["w_uk"]).reshape(2, 256, 1024)
    shared["w_uv"] = f(inp["w_uv"]).reshape(2, 256, 1024)
    shared.update(tabs)
    in_maps = []
    for c in range(8):
        m = dict(shared)
        m["xp"] = f(inp["x_prompt"][2 * c:2 * c + 2])
        m["xs"] = f(inp["x_sample"][4 * c:4 * c + 4]).reshape(64, D)
        m["cp"] = f(inp["c_prompt"][2 * c:2 * c + 2])
        m["cs"] = f(inp["c_sample"][4 * c:4 * c + 4])
        m["cckv"] = f(inp["cache_ckv"][:, 4 * c:4 * c + 4])
        m["ckpe"] = f(inp["cache_kpe"][:, 4 * c:4 * c + 4])
        m["sconv"] = f(inp["state_conv"][:, 4 * c:4 * c + 4])
        m["slru"] = f(inp["state_lru"][:, 4 * c:4 * c + 4])
        in_maps.append(m)
    res = run_bass_kernel_spmd(mk.nc, in_maps, core_ids=list(range(8)))
    R = res.results
    cat = lambda k, ax: np.concatenate([np.asarray(r[k]) for r in R], axis=ax)
    yp = cat("yp", 0)
    ys = cat("ys", 0).reshape(32, 16, D)
    nckv_p = cat("nckv_p", 1)
    nkpe_p = cat("nkpe_p", 1)
    nconv_p = cat("nconv_p", 1)
    nlru_p = cat("nlru_p", 1)
    nckv_s = np.concatenate([np.asarray(r["nckv_s"]).reshape(2, 4, 16, 256) for r in R], 1)
    nkpe_s = np.concatenate([np.asarray(r["nkpe_s"]).reshape(2, 4, 16, 64) for r in R], 1)
    nconv_s = cat("nconv_s", 1)
    nlru_s = cat("nlru_s", 1)
    return tuple(np.ascontiguousarray(a, dtype=np.float32) for a in
                 (yp, ys, nckv_p, nkpe_p, nconv_p, nlru_p, nckv_s, nkpe_s, nconv_s, nlru_s))
```
